# Optimizing a Trainium2 kernel written in Bass

```python
import math
import jax, jax.numpy as jnp
from jax import lax
import numpy as np

D_MODEL = 1024
BATCH = 8
SEQ = 2048
DEPTH = 2
DEC_BATCH = 32
DEC_SEQ = 64
PAST_LEN = 4096

CHUNK = 64
EPS = 1e-6
GLA_HEADS = 4
GLA_KEY_DIM = D_MODEL // 2
GLA_VALUE_DIM = D_MODEL
GLA_DK = GLA_KEY_DIM // GLA_HEADS
GLA_DV = GLA_VALUE_DIM // GLA_HEADS
GLA_GATE_RANK = 16
GLA_GATE_NORM = 16.0
SSD_INNER = 2 * D_MODEL
SSD_HEADDIM = 64
SSD_HEADS = SSD_INNER // SSD_HEADDIM
SSD_GROUPS = 4
SSD_DSTATE = 128
CONV_WIDTH = 4
SSD_CONV_DIM = SSD_INNER + 2 * SSD_GROUPS * SSD_DSTATE
N_BRANCH = 2
IN_SPLITS = (GLA_KEY_DIM, GLA_KEY_DIM, GLA_VALUE_DIM, GLA_VALUE_DIM, GLA_GATE_RANK,
             SSD_INNER, SSD_CONV_DIM, SSD_HEADS, N_BRANCH * D_MODEL)
D_IN_PROJ = sum(IN_SPLITS)
PEER_HEADS = 8
PEER_NKEYS = 128
PEER_EXPERTS = PEER_NKEYS * PEER_NKEYS
PEER_DQ = 256
PEER_TOPK = 16
PEER_TOKEN_BLOCK = 128

kernel_name = "hybrid_gla_ssd_peer_streaming_step"


def rmsnorm(x, w):
    xf = x.astype(jnp.float32)
    y = xf * lax.rsqrt(jnp.mean(xf * xf, axis=-1, keepdims=True) + EPS)
    return (y * w.astype(jnp.float32)).astype(x.dtype)


def chunk_len(T):
    return CHUNK if T >= CHUNK else T


def gla_chunked(q, k, v, gk, s0):
    Bsz, T, H, DK = q.shape
    DV = v.shape[-1]
    L = chunk_len(T)
    NC = T // L
    q = q.reshape(Bsz, NC, L, H, DK)
    k = k.reshape(Bsz, NC, L, H, DK)
    v = v.reshape(Bsz, NC, L, H, DV)
    b = jnp.cumsum(gk.reshape(Bsz, NC, L, H, DK), axis=2)
    b_last = b[:, :, -1]
    q_t = q * jnp.exp(b)
    k_t = k * jnp.exp(-b)
    causal = jnp.tril(jnp.ones((L, L), dtype=bool))
    att = jnp.where(causal, jnp.einsum('bnihd,bnjhd->bnhij', q_t, k_t), 0.0)
    o_intra = jnp.einsum('bnhij,bnjhe->bnihe', att, v)
    k_end = k * jnp.exp(b_last[:, :, None] - b)
    d_s = jnp.einsum('bnjhd,bnjhe->bnhde', k_end, v)

    def step(s, inp):
        dec, ds = inp
        return s * dec[..., None] + ds, s

    s_final, s_starts = lax.scan(step, s0, (jnp.moveaxis(jnp.exp(b_last), 1, 0), jnp.moveaxis(d_s, 1, 0)))
    s_starts = jnp.moveaxis(s_starts, 0, 1)
    o_inter = jnp.einsum('bnihd,bnhde->bnihe', q_t, s_starts)
    return (o_intra + o_inter).reshape(Bsz, T, H, DV), s_final


def ssd_chunked(x, dt, A, bm, cm, s0):
    Bsz, T, H, P = x.shape
    G, N = bm.shape[-2], bm.shape[-1]
    HG = H // G
    L = chunk_len(T)
    NC = T // L
    x = x.reshape(Bsz, NC, L, G, HG, P)
    dt = dt.reshape(Bsz, NC, L, G, HG)
    bm = bm.reshape(Bsz, NC, L, G, N)
    cm = cm.reshape(Bsz, NC, L, G, N)
    a = jnp.cumsum(dt * A.reshape(G, HG), axis=2)
    a_last = a[:, :, -1]
    a_t = jnp.moveaxis(a, 2, -1)
    causal = jnp.tril(jnp.ones((L, L), dtype=bool))
    decay = jnp.exp(jnp.where(causal, a_t[..., :, None] - a_t[..., None, :], -jnp.inf))
    cb = jnp.einsum('bnigs,bnjgs->bngij', cm, bm)
    w_intra = cb[:, :, :, None] * decay * jnp.moveaxis(dt, 2, -1)[..., None, :]
    y_intra = jnp.einsum('bnghij,bnjghp->bnighp', w_intra, x)
    xw = x * (dt * jnp.exp(a_last[:, :, None] - a))[..., None]
    d_s = jnp.einsum('bnjghp,bnjgs->bnghps', xw, bm)

    def step(s, inp):
        dec, ds = inp
        return s * dec[..., None, None] + ds, s

    s_final, s_starts = lax.scan(step, s0.reshape(Bsz, G, HG, P, N),
                                 (jnp.moveaxis(jnp.exp(a_last), 1, 0), jnp.moveaxis(d_s, 1, 0)))
    s_starts = jnp.moveaxis(s_starts, 0, 1)
    y_inter = jnp.einsum('bnigs,bnghps->bnighp', cm, s_starts) * jnp.exp(a)[..., None]
    y = (y_intra + y_inter).reshape(Bsz, T, H, P)
    return y, s_final.reshape(Bsz, H, P, N)


def token_mixer(h, conv_prev, gla_s0, ssd_s0, w_in, gla_gk_w2, gla_gk_b, gla_norm_w, gla_proj,
                ssd_conv_w, ssd_conv_b, ssd_dt_bias, ssd_A_log, ssd_D, ssd_norm_w, ssd_proj, w_out):
    f32 = jnp.float32
    Bsz, T, _ = h.shape
    offs = np.cumsum(IN_SPLITS)[:-1].tolist()
    q, k, v, g_out, gk_low, z, xbc, dt_raw, gate_logits = jnp.split(h @ w_in, offs, axis=-1)

    qh = q.reshape(Bsz, T, GLA_HEADS, GLA_DK).astype(f32) * (GLA_DK ** -0.5)
    kh = k.reshape(Bsz, T, GLA_HEADS, GLA_DK).astype(f32)
    vh = v.reshape(Bsz, T, GLA_HEADS, GLA_DV).astype(f32)
    gk = jax.nn.log_sigmoid((gk_low @ gla_gk_w2 + gla_gk_b).astype(f32)) / GLA_GATE_NORM
    o_a, gla_s = gla_chunked(qh, kh, vh, gk.reshape(Bsz, T, GLA_HEADS, GLA_DK), gla_s0.astype(f32))
    o_a = rmsnorm(o_a, gla_norm_w) * jax.nn.silu(g_out.reshape(Bsz, T, GLA_HEADS, GLA_DV).astype(f32))
    br_a = o_a.reshape(Bsz, T, GLA_VALUE_DIM).astype(h.dtype) @ gla_proj

    xbc_ext = jnp.concatenate([conv_prev.astype(xbc.dtype), xbc], axis=1)
    conv = ssd_conv_b + xbc_ext[:, 0:T] * ssd_conv_w[0]
    for i in range(1, CONV_WIDTH):
        conv = conv + xbc_ext[:, i:i + T] * ssd_conv_w[i]
    conv_new = xbc_ext[:, T:]
    xbc_act = jax.nn.silu(conv.astype(f32))
    xs, b_ssm, c_ssm = jnp.split(xbc_act, [SSD_INNER, SSD_INNER + SSD_GROUPS * SSD_DSTATE], axis=-1)
    xs = xs.reshape(Bsz, T, SSD_HEADS, SSD_HEADDIM)
    dt = jax.nn.softplus(dt_raw.astype(f32) + ssd_dt_bias.astype(f32))
    A = -jnp.exp(ssd_A_log.astype(f32))
    y, ssd_s = ssd_chunked(xs, dt, A,
                           b_ssm.reshape(Bsz, T, SSD_GROUPS, SSD_DSTATE),
                           c_ssm.reshape(Bsz, T, SSD_GROUPS, SSD_DSTATE),
                           ssd_s0.astype(f32))
    y = (y + ssd_D.astype(f32)[:, None] * xs).reshape(Bsz, T, SSD_INNER) * jax.nn.silu(z.astype(f32))
    y = rmsnorm(y.reshape(Bsz, T, SSD_GROUPS, SSD_INNER // SSD_GROUPS),
                ssd_norm_w.reshape(SSD_GROUPS, SSD_INNER // SSD_GROUPS)).reshape(Bsz, T, SSD_INNER)
    br_b = y.astype(h.dtype) @ ssd_proj

    g_a, g_b = jnp.split(jax.nn.sigmoid(gate_logits), N_BRANCH, axis=-1)
    return (g_a * br_a + g_b * br_b) @ w_out, conv_new, gla_s, ssd_s


def peer_ffn(h, wq, keys1, keys2, u_tab, v_tab):
    Bsz, T, D = h.shape
    xt = h.reshape(Bsz * T, D)
    n = Bsz * T
    npad = (-n) % PEER_TOKEN_BLOCK
    blocks = jnp.pad(xt, ((0, npad), (0, 0))).reshape(-1, PEER_TOKEN_BLOCK, D)
    half = PEER_DQ // 2

    def one_block(xb):
        q = (xb @ wq).reshape(PEER_TOKEN_BLOCK, PEER_HEADS, PEER_DQ).astype(jnp.float32)
        s1 = jnp.einsum('thd,hkd->thk', q[..., :half], keys1.astype(jnp.float32))
        s2 = jnp.einsum('thd,hkd->thk', q[..., half:], keys2.astype(jnp.float32))
        v1, i1 = lax.top_k(s1, PEER_TOPK)
        v2, i2 = lax.top_k(s2, PEER_TOPK)
        cand = (v1[..., :, None] + v2[..., None, :]).reshape(PEER_TOKEN_BLOCK, PEER_HEADS, PEER_TOPK * PEER_TOPK)
        cidx = (i1[..., :, None] * PEER_NKEYS + i2[..., None, :]).reshape(PEER_TOKEN_BLOCK, PEER_HEADS, PEER_TOPK * PEER_TOPK)
        sc, pos = lax.top_k(cand, PEER_TOPK)
        eidx = jnp.take_along_axis(cidx, pos, axis=-1)
        gsm = jax.nn.softmax(sc, axis=-1)
        u_sel = u_tab[eidx]
        v_sel = v_tab[eidx]
        act = jnp.einsum('td,thkd->thk', xb, u_sel).astype(jnp.float32)
        w = gsm * jax.nn.gelu(act, approximate=False)
        return jnp.einsum('thk,thkd->td', w.astype(xb.dtype), v_sel)

    out = lax.map(one_block, blocks).reshape(-1, D)[:n]
    return out.reshape(Bsz, T, D)


def run_trunk(x, c, conv_state, gla_state, ssd_state, w_ada, b_ada, norm1_w, w_in, gla_gk_w2, gla_gk_b,
              gla_norm_w, gla_proj, ssd_conv_w, ssd_conv_b, ssd_dt_bias, ssd_A_log, ssd_D, ssd_norm_w,
              ssd_proj, w_out, norm2_w, peer_wq, peer_keys1, peer_keys2, peer_u, peer_v, final_norm_w):
    new_gla, new_ssd, new_conv = [], [], []
    for l in range(DEPTH):
        mod = jax.nn.silu(c) @ w_ada[l] + b_ada[l]
        sh1, sc1, ga1, sh2, sc2, ga2 = jnp.split(mod[:, None, :], 6, axis=-1)
        h = rmsnorm(x, norm1_w[l]) * (1 + sc1) + sh1
        y, cv, sg, ss = token_mixer(h, conv_state[l], gla_state[l], ssd_state[l], w_in[l], gla_gk_w2[l],
                                    gla_gk_b[l], gla_norm_w[l], gla_proj[l], ssd_conv_w[l], ssd_conv_b[l],
                                    ssd_dt_bias[l], ssd_A_log[l], ssd_D[l], ssd_norm_w[l], ssd_proj[l], w_out[l])
        x = x + ga1 * y
        h = rmsnorm(x, norm2_w[l]) * (1 + sc2) + sh2
        x = x + ga2 * peer_ffn(h, peer_wq[l], peer_keys1[l], peer_keys2[l], peer_u[l], peer_v[l])
        new_gla.append(sg)
        new_ssd.append(ss)
        new_conv.append(cv)
    return rmsnorm(x, final_norm_w), jnp.stack(new_gla), jnp.stack(new_ssd), jnp.stack(new_conv)


def setup_inputs(seed: int = 0) -> dict:
    key = jax.random.key(seed)
    ks = iter(jax.random.split(key, 40))
    f32 = jnp.float32

    def nrm(shape, scale):
        return jax.random.normal(next(ks), shape, f32) * scale

    u_dt = jax.random.uniform(next(ks), (DEPTH, SSD_HEADS), f32)
    dt0 = jnp.exp(u_dt * (math.log(0.1) - math.log(1e-3)) + math.log(1e-3))
    return {
        "x_prompt": nrm((BATCH, SEQ, D_MODEL), 1.0),
        "x_sample": nrm((DEC_BATCH, DEC_SEQ, D_MODEL), 1.0),
        "state_gla": nrm((DEPTH, DEC_BATCH, GLA_HEADS, GLA_DK, GLA_DV), 1.0),
        "state_ssd": nrm((DEPTH, DEC_BATCH, SSD_HEADS, SSD_HEADDIM, SSD_DSTATE), 0.3),
        "state_conv": nrm((DEPTH, DEC_BATCH, CONV_WIDTH - 1, SSD_CONV_DIM), 1.0),
        "c_prompt": nrm((BATCH, D_MODEL), 1.0),
        "c_sample": nrm((DEC_BATCH, D_MODEL), 1.0),
        "w_ada": nrm((DEPTH, D_MODEL, 6 * D_MODEL), 0.5 * D_MODEL ** -0.5),
        "b_ada": nrm((DEPTH, 6 * D_MODEL), 0.02),
        "norm1_w": 1.0 + nrm((DEPTH, D_MODEL), 0.02),
        "w_in": nrm((DEPTH, D_MODEL, D_IN_PROJ), D_MODEL ** -0.5),
        "gla_gk_w2": nrm((DEPTH, GLA_GATE_RANK, GLA_KEY_DIM), GLA_GATE_RANK ** -0.5),
        "gla_gk_b": nrm((DEPTH, GLA_KEY_DIM), 0.1),
        "gla_norm_w": 1.0 + nrm((DEPTH, GLA_DV), 0.02),
        "gla_proj": nrm((DEPTH, GLA_VALUE_DIM, D_MODEL), GLA_VALUE_DIM ** -0.5),
        "ssd_conv_w": nrm((DEPTH, CONV_WIDTH, SSD_CONV_DIM), CONV_WIDTH ** -0.5),
        "ssd_conv_b": nrm((DEPTH, SSD_CONV_DIM), 0.02),
        "ssd_dt_bias": dt0 + jnp.log(-jnp.expm1(-dt0)),
        "ssd_A_log": jnp.log(jax.random.uniform(next(ks), (DEPTH, SSD_HEADS), f32, minval=1.0, maxval=16.0)),
        "ssd_D": 1.0 + nrm((DEPTH, SSD_HEADS), 0.1),
        "ssd_norm_w": 1.0 + nrm((DEPTH, SSD_INNER), 0.02),
        "ssd_proj": nrm((DEPTH, SSD_INNER, D_MODEL), SSD_INNER ** -0.5),
        "w_out": nrm((DEPTH, D_MODEL, D_MODEL), D_MODEL ** -0.5),
        "norm2_w": 1.0 + nrm((DEPTH, D_MODEL), 0.02),
        "peer_wq": nrm((DEPTH, D_MODEL, PEER_HEADS * PEER_DQ), D_MODEL ** -0.5),
        "peer_keys1": nrm((DEPTH, PEER_HEADS, PEER_NKEYS, PEER_DQ // 2), (PEER_DQ // 2) ** -0.5),
        "peer_keys2": nrm((DEPTH, PEER_HEADS, PEER_NKEYS, PEER_DQ // 2), (PEER_DQ // 2) ** -0.5),
        "peer_u": nrm((DEPTH, PEER_EXPERTS, D_MODEL), D_MODEL ** -0.5),
        "peer_v": nrm((DEPTH, PEER_EXPERTS, D_MODEL), (PEER_HEADS * PEER_TOPK) ** -0.5),
        "final_norm_w": 1.0 + nrm((D_MODEL,), 0.02),
    }


def reference(x_prompt, x_sample, state_gla, state_ssd, state_conv, c_prompt, c_sample, w_ada, b_ada,
              norm1_w, w_in, gla_gk_w2, gla_gk_b, gla_norm_w, gla_proj, ssd_conv_w, ssd_conv_b, ssd_dt_bias,
              ssd_A_log, ssd_D, ssd_norm_w, ssd_proj, w_out, norm2_w, peer_wq, peer_keys1, peer_keys2,
              peer_u, peer_v, final_norm_w):
    weights = (w_ada, b_ada, norm1_w, w_in, gla_gk_w2, gla_gk_b, gla_norm_w, gla_proj, ssd_conv_w,
               ssd_conv_b, ssd_dt_bias, ssd_A_log, ssd_D, ssd_norm_w, ssd_proj, w_out, norm2_w, peer_wq,
               peer_keys1, peer_keys2, peer_u, peer_v, final_norm_w)
    nb = x_prompt.shape[0]
    zero_conv = jnp.zeros((DEPTH, nb, CONV_WIDTH - 1, SSD_CONV_DIM), x_prompt.dtype)
    zero_gla = jnp.zeros((DEPTH, nb, GLA_HEADS, GLA_DK, GLA_DV), jnp.float32)
    zero_ssd = jnp.zeros((DEPTH, nb, SSD_HEADS, SSD_HEADDIM, SSD_DSTATE), jnp.float32)
    y_prompt, gla_p, ssd_p, conv_p = run_trunk(x_prompt, c_prompt, zero_conv, zero_gla, zero_ssd, *weights)
    y_sample, gla_s, ssd_s, conv_s = run_trunk(x_sample, c_sample, state_conv, state_gla, state_ssd, *weights)
    return (y_prompt, y_sample, gla_p, ssd_p, conv_p, gla_s, ssd_s, conv_s)
```

```python
from contextlib import ExitStack
import numpy as np
import concourse.bass as bass
import concourse.mybir as mybir

F32 = mybir.dt.float32
BF16 = mybir.dt.bfloat16
I32 = mybir.dt.int32
U32 = mybir.dt.uint32
AF = mybir.ActivationFunctionType
ALU = mybir.AluOpType
AX = mybir.AxisListType

ENGS = ("pe", "act", "dve", "pool", "sp")
SKIP_SAME = ("pe", "act")


class V:
    __slots__ = ("ap", "buf")

    def __init__(self, ap, buf):
        self.ap = ap
        self.buf = buf

    def __getitem__(self, k):
        return V(self.ap[k], self.buf)

    def rearrange(self, pat_, **kw):
        return V(self.ap.rearrange(pat_, **kw), self.buf)

    def unsqueeze(self, a):
        return V(self.ap.unsqueeze(a), self.buf)

    def bc(self, shape):
        return V(self.ap.to_broadcast(list(shape)), self.buf)

    def pbc(self, n):
        return V(self.ap.partition_broadcast(n), self.buf)

    def bitcast(self, dt):
        return V(self.ap.bitcast(dt), self.buf)


class Buf:
    __slots__ = ("t", "name", "w", "r")

    def __init__(self, t, name):
        self.t = t
        self.name = name
        self.w = None
        self.r = []

    def __getitem__(self, k):
        return V(self.t[k], self)


def _aps(x):
    return x.ap if isinstance(x, V) else x


class Prog:
    def __init__(self, nc, n_dma_sems=24):
        self.nc = nc
        self.es = ExitStack()
        self.thunks = {e: [] for e in ENGS}
        self.sems = {}
        self.cnt = {}
        for e in ENGS:
            self.sems[e] = self.es.enter_context(nc.semaphore("s_" + e))
            self.cnt[e] = 0
        self.dq = {}
        for q, n in (("sp", n_dma_sems), ("pool", n_dma_sems)):
            lst = []
            for i in range(n):
                key = "d_%s_%d" % (q, i)
                self.sems[key] = self.es.enter_context(nc.semaphore(key))
                self.cnt[key] = 0
                lst.append(key)
            self.dq[q] = [lst, 0]
        self.seen = {e: {} for e in ENGS}
        self.n_instr = 0
        self.n_wait = 0
        self.scopes = []

    def _stack(self):
        return self.scopes[-1] if self.scopes else self.es

    def sb(self, name, shape, dtype):
        self.uid = getattr(self, "uid", 0) + 1
        name = "%s_%d" % (name, self.uid)
        t = self._stack().enter_context(self.nc.sbuf_tensor(name, list(shape), dtype))
        return Buf(t, name)

    def ps(self, name, shape, dtype=F32):
        t = self._stack().enter_context(self.nc.psum_tensor(name, list(shape), dtype))
        return Buf(t, name)

    def dram(self, name, shape, dtype, kind="Internal"):
        t = self.nc.dram_tensor(name, list(shape), dtype, kind=kind)
        return Buf(t.ap(), name)

    def push_scope(self):
        self.scopes.append(ExitStack())

    def pop_scope(self):
        self.barrier()
        self.scopes.pop().close()

    def barrier(self):
        for e in ENGS:
            waits = {}
            for key, val in self.cnt.items():
                if val > 0 and key != e and self.seen[e].get(key, 0) < val:
                    waits[key] = val
            self._emit_waits(e, waits)

    def _need(self, eng, ev, waits):
        if ev is None:
            return
        key, val, src = ev
        if src == eng and eng in SKIP_SAME:
            return
        if self.seen[eng].get(key, 0) >= val:
            return
        if waits.get(key, 0) < val:
            waits[key] = val

    def _deps(self, eng, reads, writes):
        waits = {}
        for b in reads:
            self._need(eng, b.w, waits)
        for b in writes:
            self._need(eng, b.w, waits)
            for ev in b.r:
                self._need(eng, ev, waits)
        return waits

    def _emit_waits(self, eng, waits):
        for key, val in waits.items():
            sem = self.sems[key]
            self.thunks[eng].append(lambda E, sem=sem, val=val: E.wait_ge(sem, val))
            self.seen[eng][key] = val
            self.n_wait += 1

    def _record(self, ev, reads, writes):
        for b in writes:
            b.w = ev
            b.r = []
        for b in reads:
            if b.w is not ev:
                b.r.append(ev)
                if len(b.r) > 16:
                    best = {}
                    for k, v, s in b.r:
                        if best.get(k, (0, None))[0] < v:
                            best[k] = (v, s)
                    b.r = [(k, v, s) for k, (v, s) in best.items()]

    def op(self, eng, fn, reads=(), writes=()):
        reads = [x.buf if isinstance(x, V) else x for x in reads]
        writes = [x.buf if isinstance(x, V) else x for x in writes]
        waits = self._deps(eng, reads, writes)
        self._emit_waits(eng, waits)
        self.cnt[eng] += 1
        val = self.cnt[eng]
        sem = self.sems[eng]
        self.thunks[eng].append(lambda E, fn=fn, sem=sem: fn(E).then_inc(sem, 1))
        self._record((eng, val, eng), reads, writes)
        self.n_instr += 1

    def dma(self, q, out, in_, **kw):
        reads = [in_.buf]
        writes = [out.buf]
        waits = self._deps(q, reads, writes)
        lst, idx = self.dq[q]
        key = lst[idx % len(lst)]
        self.dq[q][1] = idx + 1
        if self.cnt[key] > 0 and self.seen[q].get(key, 0) < self.cnt[key]:
            if waits.get(key, 0) < self.cnt[key]:
                waits[key] = self.cnt[key]
        self._emit_waits(q, waits)
        self.cnt[key] += 16
        val = self.cnt[key]
        sem = self.sems[key]
        self.thunks[q].append(
            lambda E, o=out.ap, i=in_.ap, sem=sem, kw=kw: E.dma_start(out=o, in_=i, **kw).then_inc(sem, 16))
        self._record((key, val, "dma_" + q), reads, writes)
        self.n_instr += 1

    def mm(self, o, l, r, start=True, stop=True):
        self.op("pe", lambda E, o=o.ap, l=l.ap, r=r.ap: E.matmul(o, l, r, start=start, stop=stop),
                reads=[l, r], writes=[o])

    def tr(self, o, i, ident):
        self.op("pe", lambda E, o=o.ap, i=i.ap, d=ident.ap: E.transpose(o, i, d), reads=[i, ident], writes=[o])

    def act(self, o, i, func, bias=None, scale=None, accum=None, eng="act"):
        rd = [i]
        kw = {}
        if bias is not None:
            kw["bias"] = _aps(bias)
            if isinstance(bias, V):
                rd.append(bias)
        if scale is not None:
            kw["scale"] = _aps(scale)
            if isinstance(scale, V):
                rd.append(scale)
        wr = [o]
        if accum is not None:
            kw["accum_out"] = accum.ap
            wr.append(accum)
        self.op(eng, lambda E, o=o.ap, i=i.ap, kw=kw: E.activation(o, i, func, **kw), reads=rd, writes=wr)

    def tt(self, eng, o, a, b, op):
        self.op(eng, lambda E, o=o.ap, a=a.ap, b=b.ap: E.tensor_tensor(o, a, b, op), reads=[a, b], writes=[o])

    def ts(self, eng, o, a, s1, s2, op0, op1=None, accum=None):
        rd = [a] + [s for s in (s1, s2) if isinstance(s, V)]
        kw = {}
        wr = [o]
        if op1 is not None:
            kw["op1"] = op1
        if accum is not None:
            kw["accum_out"] = accum.ap
            wr.append(accum)
        self.op(eng, lambda E, o=o.ap, a=a.ap, s1=_aps(s1), s2=_aps(s2), kw=kw:
                E.tensor_scalar(o, a, s1, s2, op0, **kw), reads=rd, writes=wr)

    def stt(self, eng, o, a, s, b, op0, op1, accum=None):
        rd = [a, b] + ([s] if isinstance(s, V) else [])
        kw = {}
        wr = [o]
        if accum is not None:
            kw["accum_out"] = accum.ap
            wr.append(accum)
        self.op(eng, lambda E, o=o.ap, a=a.ap, s=_aps(s), b=b.ap, kw=kw:
                E.scalar_tensor_tensor(o, a, s, b, op0, op1, **kw), reads=rd, writes=wr)

    def cp(self, eng, o, i):
        if eng == "act":
            self.op(eng, lambda E, o=o.ap, i=i.ap: E.copy(o, i), reads=[i], writes=[o])
        else:
            self.op(eng, lambda E, o=o.ap, i=i.ap: E.tensor_copy(o, i), reads=[i], writes=[o])

    def memset(self, eng, o, val):
        self.op(eng, lambda E, o=o.ap: E.memset(o, val), writes=[o])

    def finish(self, out_bufs):
        waits = {}
        for b in out_bufs:
            self._need("sp", b.w, waits)
        self._emit_waits("sp", waits)

    def run_block(self):
        nc = self.nc
        names = {"pe": "tensor", "act": "scalar", "dve": "vector", "pool": "gpsimd", "sp": "sync"}
        with nc.Block() as block:
            for e in ENGS:
                th = self.thunks[e]

                def body(E, th=th):
                    for f in th:
                        f(E)
                getattr(block, names[e])(body)

    def close(self):
        while self.scopes:
            self.scopes.pop().close()
        self.es.close()
D = 1024
EPS = 1e-6
OFF_Q, OFF_K, OFF_V, OFF_GO, OFF_GKL, OFF_Z, OFF_XBC, OFF_DT, OFF_GATE = 0, 512, 1024, 2048, 3072, 3088, 5136, 8208, 8240
DIN = 10288


class Cfg:
    def __init__(self, TP, NS, DEPTH, do_peer=True):
        self.TP, self.NS, self.DEPTH = TP, NS, DEPTH
        self.NSEQ = 1 + NS
        self.NT = TP + 64 * NS
        assert self.NT % 128 == 0 and TP % 128 == 0
        self.NCH = self.NT // 64
        self.NTL = self.NT // 128
        self.do_peer = do_peer

    def seq_of_chunk(self, c):
        return 0 if c < self.TP // 64 else 1 + (c - self.TP // 64)

    def first_chunk(self, c):
        return c == 0 or c >= self.TP // 64

    def last_chunk(self, c):
        return c == self.TP // 64 - 1 or c >= self.TP // 64

    def segments(self, t0, t1):
        out = []
        t = t0
        while t < t1:
            if t < self.TP:
                e = min(t1, self.TP)
                out.append((t, e - t, 0))
            else:
                s = 1 + (t - self.TP) // 64
                e = min(t1, self.TP + (s) * 64)
                out.append((t, e - t, s))
            t = e
        return out


def build(nc, cfg):
    P = Prog(nc)
    NT, NSEQ, NS, NCH, NTL, DEPTH = cfg.NT, cfg.NSEQ, cfg.NS, cfg.NCH, cfg.NTL, cfg.DEPTH
    ein = lambda n, s, dt=F32: P.dram(n, s, dt, kind="ExternalInput")
    eout = lambda n, s, dt=F32: P.dram(n, s, dt, kind="ExternalOutput")
    x_d = ein("x", [NT, D])
    c_d = ein("cvec", [NSEQ, D])
    sgla_d = ein("sgla", [DEPTH, NS, 4, 128, 256])
    sssd_d = ein("sssd", [DEPTH, NS, 2048, 128])
    sconv_d = ein("sconv", [DEPTH, NS, 3, 3072])
    w_ada = ein("w_ada", [DEPTH, D, 6 * D])
    b_ada = ein("b_ada", [DEPTH, 48, 128])
    norm1_w = ein("norm1_w", [DEPTH, 8, 128])
    w_in = ein("w_in", [DEPTH, D, DIN])
    gk_w2 = ein("gla_gk_w2", [DEPTH, 16, 512])
    gk_b = ein("gla_gk_b", [DEPTH, 4, 128])
    gla_norm_w = ein("gla_norm_w", [DEPTH, 256])
    gla_proj = ein("gla_proj", [DEPTH, D, D])
    conv_w = ein("ssd_conv_w", [DEPTH, 4, 24, 128])
    conv_b = ein("ssd_conv_b", [DEPTH, 24, 128])
    dt_bias = ein("ssd_dt_bias", [DEPTH, 32])
    A_log = ein("ssd_A_log", [DEPTH, 32])
    ssd_D = ein("ssd_D", [DEPTH, 32])
    ssd_norm_w = ein("ssd_norm_w", [DEPTH, 2048])
    ssd_proj = ein("ssd_proj", [DEPTH, 2048, D])
    w_out = ein("w_out", [DEPTH, D, D])
    norm2_w = ein("norm2_w", [DEPTH, 8, 128])
    peer_wq = ein("peer_wq", [DEPTH, D, 2048])
    keys1 = ein("peer_keys1", [DEPTH, 8, 128, 128])
    keys2 = ein("peer_keys2", [DEPTH, 8, 128, 128])
    peer_u = ein("peer_u", [DEPTH, 128, 128, D])
    peer_v = ein("peer_v", [DEPTH, 128, 128, D])
    fnorm_w = ein("final_norm_w", [8, 128])
    y_d = eout("y", [NT, D])
    ogla_d = eout("ogla", [DEPTH, NSEQ, 4, 128, 256])
    ossd_d = eout("ossd", [DEPTH, NSEQ, 2048, 128])
    oconv_d = eout("oconv", [DEPTH, NSEQ, 3, 3072])
    outs = [y_d, ogla_d, ossd_d, oconv_d]
    xT_d = P.dram("xT_s", [D, NT], F32)
    featT_d = P.dram("featT_s", [6144, NT], BF16)
    gkl_d = P.dram("gkl_s", [16, NT], F32)
    tokM_d = P.dram("tokM_s", [NT, 4096], BF16)
    dt_d = P.dram("dt_s", [NT, 32], F32)
    uT_d = P.dram("uT_s", [128, 128, 8, 128], BF16)
    vb_d = P.dram("vb_s", [128, 128, D], BF16)

    ident_f = P.sb("ident_f", [128, 128], F32)
    ident_b = P.sb("ident_b", [128, 128], BF16)
    ones_f = P.sb("ones_f", [128, 128], F32)
    triU = P.sb("triU", [64, 64], F32)
    Lst = P.sb("Lst", [64, 64], F32)
    iota_f = P.sb("iota_f", [128, 128], F32)
    epsb = P.sb("epsb", [128, 1], F32)
    modA1 = P.sb("modA1", [128, DEPTH, 8, NSEQ], F32)
    modB1 = P.sb("modB1", [128, DEPTH, 8, NSEQ], F32)
    modG1 = P.sb("modG1", [128, DEPTH, 8, NSEQ], F32)
    modA2 = P.sb("modA2", [128, DEPTH, 8, NSEQ], F32)
    modB2 = P.sb("modB2", [128, DEPTH, 8, NSEQ], F32)
    modG2 = P.sb("modG2", [128, DEPTH, 8, NSEQ], F32)
    fnw = P.sb("fnw", [128, 8], F32)
    zeroc = P.sb("zeroc", [128, 1], F32)
    negsix = P.sb("negsix", [128, 1], F32)
    neghalf = P.sb("neghalf", [128, 1], F32)
    zer64 = P.sb("zer64", [128, 64], F32)
    hTbox = [None]
    pf = [P.ps("pf%d" % i, [128, 512], F32) for i in range(6)]
    pb = [P.ps("pb%d" % i, [128, 1024], BF16) for i in range(2)]

    P.memset("pool", ident_f[:], 0.0)
    P.op("pool", lambda E: E.affine_select(ident_f[:].ap, ident_f[:].ap, pattern=[[-1, 128]], compare_op=ALU.not_equal,
                                           fill=1.0, base=0, channel_multiplier=1), reads=[ident_f], writes=[ident_f])
    P.cp("dve", ident_b[:], ident_f[:])
    P.memset("pool", ones_f[:], 1.0)
    P.memset("pool", triU[:], 1.0)
    P.op("pool", lambda E: E.affine_select(triU[:].ap, triU[:].ap, pattern=[[1, 64]], compare_op=ALU.is_ge,
                                           fill=0.0, base=0, channel_multiplier=-1), reads=[triU], writes=[triU])
    P.memset("pool", Lst[:], 1.0)
    P.op("pool", lambda E: E.affine_select(Lst[:].ap, Lst[:].ap, pattern=[[-1, 64]], compare_op=ALU.is_ge,
                                           fill=0.0, base=-1, channel_multiplier=1), reads=[Lst], writes=[Lst])
    P.op("pool", lambda E: E.iota(iota_f[:].ap, pattern=[[1, 128]], base=0, channel_multiplier=0,
                                  allow_small_or_imprecise_dtypes=True), writes=[iota_f])
    P.memset("pool", epsb[:], EPS)
    P.memset("pool", zeroc[:], 0.0)
    P.memset("pool", negsix[:], -1.0 / 16.0)
    P.memset("pool", neghalf[:], -0.5)
    P.memset("pool", zer64[:], 0.0)

    rr = [0]

    def evac(o, i):
        rr[0] += 1
        P.cp("act" if rr[0] % 2 else "dve", o, i)

    def loadT(dst, src, n, tmp, pbank):
        P.dma("sp", tmp[0:n, 0:128], src)
        P.tr(pbank[:, 0:n], tmp[0:n, 0:128], ident_f[0:n, 0:n])
        P.cp("dve", dst, pbank[:, 0:n])

    P.push_scope()
    xin = [P.sb("xin%d" % i, [128, D], F32) for i in range(2)]
    xTs = [P.sb("xTs%d" % i, [128, 8, 128], F32) for i in range(2)]
    for t in range(NTL):
        xi = xin[t % 2]
        xo = xTs[t % 2]
        P.dma("sp", xi[:], x_d[t * 128:(t + 1) * 128, :])
        for k in range(8):
            pk = pf[k % 4]
            P.tr(pk[:, 0:128], xi[:, k * 128:(k + 1) * 128], ident_f[:])
            evac(xo[:, k, :], pk[:, 0:128])
        P.dma("sp", xT_d[:].rearrange("(k p) t -> p k t", p=128)[:, :, t * 128:(t + 1) * 128], xo[:])
    P.pop_scope()

    P.push_scope()
    tmpT = P.sb("tmpT", [128, 128], F32)
    cT = P.sb("cT", [128, 8, NSEQ], F32)
    siluT = P.sb("siluT", [128, 8, NSEQ], F32)
    badaT = P.sb("badaT", [128, 48], F32)
    n1T = P.sb("n1T", [128, 8], F32)
    n2T = P.sb("n2T", [128, 8], F32)
    modT = P.sb("modT", [128, 48, NSEQ], F32)
    wada = [P.sb("wada%d" % i, [128, 8, 768], F32) for i in range(2)]
    for k in range(8):
        loadT(cT[:, k, :], c_d[:, k * 128:(k + 1) * 128], NSEQ, tmpT, pf[4])
    P.act(siluT[:], cT[:], AF.Silu)
    loadT(fnw[:], fnorm_w[:, :], 8, tmpT, pf[4])
    for l in range(DEPTH):
        loadT(badaT[:], b_ada[l], 48, tmpT, pf[4])
        loadT(n1T[:], norm1_w[l], 8, tmpT, pf[4])
        loadT(n2T[:], norm2_w[l], 8, tmpT, pf[4])
        for blk in range(8):
            wb_ = wada[blk % 2]
            P.dma("sp", wb_[:], w_ada[l].rearrange("(k p) n -> p k n", p=128)[:, :, blk * 768:(blk + 1) * 768])
            for jj in range(6):
                j = blk * 6 + jj
                for k in range(8):
                    P.mm(pf[5][:, j * NSEQ:(j + 1) * NSEQ], wb_[:, k, jj * 128:(jj + 1) * 128], siluT[:, k, :],
                         start=(k == 0), stop=(k == 7))
        P.tt("dve", modT[:], pf[5][:, 0:48 * NSEQ].rearrange("p (j s) -> p j s", s=NSEQ),
             badaT[:].unsqueeze(2).bc([128, 48, NSEQ]), ALU.add)
        P.stt("dve", modA1[:, l], modT[:, 8:16, :], 1.0, n1T[:].unsqueeze(2).bc([128, 8, NSEQ]), ALU.add, ALU.mult)
        P.cp("dve", modB1[:, l], modT[:, 0:8, :])
        P.cp("dve", modG1[:, l], modT[:, 16:24, :])
        P.stt("dve", modA2[:, l], modT[:, 32:40, :], 1.0, n2T[:].unsqueeze(2).bc([128, 8, NSEQ]), ALU.add, ALU.mult)
        P.cp("dve", modB2[:, l], modT[:, 24:32, :])
        P.cp("dve", modG2[:, l], modT[:, 40:48, :])
    P.pop_scope()

    def norm_mod(A, B, l):
        P.push_scope()
        xg = [P.sb("xg%d" % i, [128, 8, 512], F32) for i in range(2)]
        sq = P.sb("sq", [128, 8, 512], F32)
        rstd = P.sb("rstd", [128, 512], F32)
        tmpn = [P.sb("tmpn%d" % i, [128, 512], F32) for i in range(2)]
        gi = 0
        for t0 in range(0, NT, 512):
            n = min(512, NT - t0)
            xv = xg[gi % 2]
            gi += 1
            P.dma("sp", xv[:, :, 0:n], xT_d[:].rearrange("(k p) t -> p k t", p=128)[:, :, t0:t0 + n])
            P.act(sq[:, :, 0:n], xv[:, :, 0:n], AF.Square)
            for k in range(8):
                P.mm(pf[0][:, 0:n], ones_f[:], sq[:, k, 0:n], start=(k == 0), stop=(k == 7))
            P.act(rstd[:, 0:n], pf[0][:, 0:n], AF.Sqrt, bias=epsb[:, 0:1], scale=1.0 / D)
            P.op("dve", lambda E, o=rstd[:, 0:n].ap: E.reciprocal(o, o), reads=[rstd], writes=[rstd])
            for k in range(8):
                tm = tmpn[k % 2]
                P.tt("dve", tm[:, 0:n], xv[:, k, 0:n], rstd[:, 0:n], ALU.mult)
                for (s0, ln, sq_) in cfg.segments(t0, t0 + n):
                    P.act(hTbox[0][:, k, s0:s0 + ln], tm[:, s0 - t0:s0 - t0 + ln], AF.Identity,
                          bias=B[:, l, k, sq_:sq_ + 1], scale=A[:, l, k, sq_:sq_ + 1])
        P.pop_scope()

    def in_proj(l):
        hT = hTbox[0]
        P.push_scope()
        wblk = [P.sb("wblk%d" % i, [128, 8, 512], BF16) for i in range(2)]
        wsm = P.sb("wsm", [128, 8, 48], BF16)
        stg = [P.sb("stg%d" % i, [128, NT], BF16) for i in range(2)]
        stgM = P.sb("stgM", [128, NTL, 512], BF16)
        stgk = P.sb("stgk", [16, NT], F32)
        stgd = P.sb("stgd", [128, NTL, 32], F32)
        wl = w_in[l].rearrange("(k p) n -> p k n", p=128)
        fblocks = [(OFF_Q, 0), (OFF_K, 512)] + [(OFF_XBC + 512 * i, 1024 + 512 * i) for i in range(6)] + \
                  [(OFF_GATE + 512 * i, 4096 + 512 * i) for i in range(4)]
        bi = 0
        pi = 0
        si = 0
        for (c0, r0) in fblocks:
            wv = wblk[bi % 2]
            bi += 1
            P.dma("pool", wv[:], wl[:, :, c0:c0 + 512])
            for j in range(4):
                st = stg[si % 2]
                si += 1
                for t0 in range(0, NT, 512):
                    n = min(512, NT - t0)
                    pk = pf[pi % 6]
                    pi += 1
                    for k in range(8):
                        P.mm(pk[:, 0:n], wv[:, k, j * 128:(j + 1) * 128], hT[:, k, t0:t0 + n], start=(k == 0), stop=(k == 7))
                    evac(st[:, t0:t0 + n], pk[:, 0:n])
                P.dma("sp", featT_d[r0 + j * 128:r0 + (j + 1) * 128, :], st[:])
        P.dma("pool", wsm[:, :, 0:16], wl[:, :, OFF_GKL:OFF_GKL + 16])
        P.dma("pool", wsm[:, :, 16:48], wl[:, :, OFF_DT:OFF_DT + 32])
        for t0 in range(0, NT, 512):
            n = min(512, NT - t0)
            pk = pf[pi % 6]
            pi += 1
            for k in range(8):
                P.mm(pk[0:16, 0:n], wsm[:, k, 0:16], hT[:, k, t0:t0 + n], start=(k == 0), stop=(k == 7))
            evac(stgk[:, t0:t0 + n], pk[0:16, 0:n])
        P.dma("sp", gkl_d[:, :], stgk[:])
        for t in range(NTL):
            pk = pf[pi % 6]
            pi += 1
            for k in range(8):
                P.mm(pk[:, 0:32], hT[:, k, t * 128:(t + 1) * 128], wsm[:, k, 16:48], start=(k == 0), stop=(k == 7))
            evac(stgd[:, t, :], pk[:, 0:32])
        dtb128 = P.sb("dtb128", [128, 32], F32)
        P.dma("sp", dtb128[:], dt_bias[l].pbc(128))
        P.tt("dve", stgd[:], stgd[:], dtb128[:].unsqueeze(1).bc([128, NTL, 32]), ALU.add)
        P.act(stgd[:], stgd[:], AF.Exp)
        P.act(stgd[:], stgd[:], AF.Ln, bias=ones_f[:, 0:1], scale=1.0)
        P.dma("sp", dt_d[:].rearrange("(t p) c -> p t c", p=128), stgd[:])
        mblocks = [(OFF_V + 512 * i, 512 * i) for i in range(2)] + [(OFF_GO + 512 * i, 1024 + 512 * i) for i in range(2)] + \
                  [(OFF_Z + 512 * i, 2048 + 512 * i) for i in range(4)]
        for (c0, m0) in mblocks:
            wv = wblk[bi % 2]
            bi += 1
            P.dma("pool", wv[:], wl[:, :, c0:c0 + 512])
            for t in range(NTL):
                pk = pf[pi % 6]
                pi += 1
                for k in range(8):
                    P.mm(pk[:, :], hT[:, k, t * 128:(t + 1) * 128], wv[:, k, :], start=(k == 0), stop=(k == 7))
                evac(stgM[:, t, :], pk[:, :])
            P.dma("sp", tokM_d[:].rearrange("(t p) c -> p t c", p=128)[:, :, m0:m0 + 512], stgM[:])
        P.pop_scope()

    def mixers(l):
        P.push_scope()
        tmpT = P.sb("tmpT2", [128, 128], F32)
        wgp = P.sb("wgp", [128, 8, D], BF16)
        wsp = P.sb("wsp", [128, 16, D], BF16)
        wop = P.sb("wop", [128, 8, D], BF16)
        P.dma("pool", wgp[:], gla_proj[l].rearrange("(k p) n -> p k n", p=128))
        P.dma("pool", wsp[:], ssd_proj[l].rearrange("(k p) n -> p k n", p=128))
        P.dma("pool", wop[:], w_out[l].rearrange("(k p) n -> p k n", p=128))
        w2 = P.sb("w2", [16, 512], F32)
        P.dma("sp", w2[:], gk_w2[l])
        gkbT = P.sb("gkbT", [128, 4], F32)
        ngkbT = P.sb("ngkbT", [128, 4], F32)
        loadT(gkbT[:], gk_b[l], 4, tmpT, pf[4])
        P.ts("dve", ngkbT[:], gkbT[:], -1.0, None, ALU.mult)
        cwT = P.sb("cwT", [128, 4, 24], F32)
        cbT_ = P.sb("cbT_", [128, 24], F32)
        for i in range(4):
            loadT(cwT[:, i, :], conv_w[l, i], 24, tmpT, pf[4])
        loadT(cbT_[:], conv_b[l], 24, tmpT, pf[4])
        gnw = P.sb("gnw", [64, 256], F32)
        P.dma("sp", gnw[:], gla_norm_w[l].pbc(64))
        snw = P.sb("snw", [64, 2048], F32)
        P.dma("sp", snw[:], ssd_norm_w[l].pbc(64))
        dtb = P.sb("dtb", [64, 32], F32)
        P.dma("sp", dtb[:], dt_bias[l].pbc(64))
        Aneg = P.sb("Aneg", [64, 32], F32)
        P.dma("sp", Aneg[:], A_log[l].pbc(64))
        P.act(Aneg[:], Aneg[:], AF.Exp)
        P.ts("dve", Aneg[:], Aneg[:], -1.0, None, ALU.mult)
        Dbc = P.sb("Dbc", [64, 32], F32)
        P.dma("sp", Dbc[:], ssd_D[l].pbc(64))
        Sg = P.sb("Sg", [128, 4, 256], F32)
        Sgb = P.sb("Sgb", [128, 4, 256], BF16)
        Ss = P.sb("Ss", [128, 2048], F32)
        Ssb = P.sb("Ssb", [128, 2048], BF16)
        cst = P.sb("cst", [72, 128], F32)
        qk = P.sb("qk", [128, 8, 64], BF16)
        gkl = P.sb("gkl", [16, 64], F32)
        vg = P.sb("vg", [64, 2048], BF16)
        zt = P.sb("zt", [64, 2048], BF16)
        dtr = P.sb("dtr", [64, 32], F32)
        xbc = P.sb("xbc", [128, 24, 67], BF16)
        ee = P.sb("ee", [128, 4, 64], F32)
        ll = P.sb("ll", [128, 4, 64], F32)
        bcs = P.sb("bcs", [128, 4, 64], F32)
        epos = P.sb("epos", [128, 4, 64], F32)
        eneg = P.sb("eneg", [128, 4, 64], F32)
        qt = P.sb("qt", [128, 4, 64], BF16)
        ktf = P.sb("ktf", [128, 4, 64], F32)
        kt = P.sb("kt", [128, 4, 64], BF16)
        kendT = P.sb("kendT", [128, 4, 64], BF16)
        kend = P.sb("kend", [64, 4, 128], BF16)
        attT = P.sb("attT", [64, 4, 64], BF16)
        ssg = P.sb("ssg", [64, 4], F32)
        sgo = P.sb("sgo", [64, 1024], BF16)
        oa = P.sb("oa", [64, 1024], BF16)
        oaT = P.sb("oaT", [128, 8, 128], BF16)
        yT = P.sb("yT", [128, 16, 128], BF16)
        xact = P.sb("xact", [128, 24, 64], BF16)
        xs_tm = P.sb("xs_tm", [64, 2048], BF16)
        B_tm = P.sb("B_tm", [64, 512], BF16)
        dtv = P.sb("dtv", [64, 32], F32)
        dtA = P.sb("dtA", [64, 32], F32)
        expa = P.sb("expa", [64, 32], F32)
        wend = P.sb("wend", [64, 32], F32)
        decS = P.sb("decS", [128, 32], F32)
        rhsD = P.sb("rhsD", [64, 32, 64], F32)
        E_ = rhsD
        cbm = P.sb("cbm", [64, 4, 64], F32)
        WT = P.sb("WT", [64, 32, 64], BF16)
        xdt = P.sb("xdt", [64, 2048], BF16)
        xw = P.sb("xw", [64, 2048], BF16)
        ytmp = P.sb("ytmp", [64, 512], F32)
        yvb = P.sb("yv", [128, 2048], F32)
        ssy = P.sb("ssy", [64, 4], F32)
        yn = P.sb("yn", [64, 2048], BF16)
        gts = P.sb("gts", [128, 16, 128], BF16)
        gsg = P.sb("gsg", [128, 16, 128], BF16)
        bra = P.sb("bra", [128, 8, 128], BF16)
        mT = P.sb("mT", [128, 8, 128], BF16)
        xt_ = P.sb("xt_", [128, 8, 128], F32)
        cvo = P.sb("cvo", [128, 3, 24], F32)
        cvs = P.sb("cvs", [72, 128], F32)
        yv = yvb[0:64, :]
        y2b = P.sb("y2", [128, 2048], F32)
        y2 = y2b[0:64, :]
        sz = y2
        on = P.sb("on", [64, 1024], BF16)
        junkg = ee[0:64, :, :].rearrange("p h t -> p (h t)")
        junk = ytmp
        dtmp = y2b[:, 0:13 * 64].rearrange("p (a t) -> p a t", t=64)
        cacc = yvb[:, 0:1536].rearrange("p (a t) -> p a t", t=64)
        cacc2 = P.sb("cacc2", [128, 11, 64], F32)
        ptmp = P.sb("ptmp", [128, 11, 64], F32)
        sst = yvb[:, :].rearrange("p (a n) -> p a n", n=128)
        featq = featT_d[0:1024, :].rearrange("(h p) t -> p h t", p=128)
        featx = featT_d[1024:4096, :].rearrange("(h p) t -> p h t", p=128)
        featg = featT_d[4096:6144, :].rearrange("(h p) t -> p h t", p=128)

        def gla_chain(c, s, t0, half):
            for h in range(4):
                P.mm(pf[0][:, h * 64:(h + 1) * 64], w2[:, h * 128:(h + 1) * 128], gkl[:], start=True, stop=True)
            yield
            for h in range(4):
                P.act(ee[:, h, :], pf[0][:, h * 64:(h + 1) * 64], AF.Exp, bias=ngkbT[:, h:h + 1], scale=-1.0)
            P.act(ll[:], ee[:], AF.Ln, bias=ones_f[:, 0:1], scale=1.0)
            yield
            for h in range(4):
                P.op("dve", lambda E, o=bcs[:, h, :].ap, a=ones_f[:, 0:64].ap, b=ll[:, h, :].ap:
                     E.tensor_tensor_scan(o, a, b, 0.0, ALU.mult, ALU.add), reads=[ones_f, ll], writes=[bcs])
            yield
            P.act(epos[:], bcs[:], AF.Exp, scale=-1.0 / 16.0)
            P.act(eneg[:], bcs[:], AF.Exp, scale=1.0 / 16.0)
            yield
            P.stt("dve", qt[:], qk[:, 0:4, :], float(128 ** -0.5), epos[:], ALU.mult, ALU.mult)
            P.tt("dve", ktf[:], qk[:, 4:8, :], eneg[:], ALU.mult)
            yield
            P.cp("act", kt[:], ktf[:])
            P.tt("dve", kendT[:], ktf[:], epos[:, :, 63:64].bc([128, 4, 64]), ALU.mult)
            yield
            for h in range(4):
                P.mm(pf[0][0:64, 256 + h * 64:256 + (h + 1) * 64], kt[:, h, :], qt[:, h, :], start=True, stop=True)
            for h in range(4):
                P.tr(pb[0][0:64, h * 128:(h + 1) * 128], kendT[:, h, :], ident_b[:])
            yield
            P.tt("dve", attT[:], pf[0][0:64, 256:512].rearrange("p (h i) -> p h i", i=64),
                 triU[:].unsqueeze(1).bc([64, 4, 64]), ALU.mult)
            P.cp("act", kend[:], pb[0][0:64, 0:512].rearrange("p (h d) -> p h d", d=128))
            P.act(sgo[:], vg[:, 1024:2048], AF.Silu)
            yield
            for h in range(4):
                ob = pf[1 + h // 2][0:64, (h % 2) * 256:(h % 2 + 1) * 256]
                P.mm(ob, attT[:, h, :], vg[:, h * 256:(h + 1) * 256], start=True, stop=False)
                P.mm(ob, qt[:, h, :], Sgb[:, h, :], start=False, stop=True)
            yield
            P.memset("pool", ssg[:], 0.0)
            for h in range(4):
                ob = pf[1 + h // 2][0:64, (h % 2) * 256:(h % 2 + 1) * 256]
                P.act(junkg, ob, AF.Square, accum=ssg[:, h:h + 1])
            yield
            P.ts("dve", ssg[:], ssg[:], 1.0 / 256, EPS, ALU.mult, ALU.add)
            P.tt("pool", ssg[:], ssg[:], neghalf[0:64, 0:1].bc([64, 4]), ALU.pow)
            yield
            for h in range(4):
                ob = pf[1 + h // 2][0:64, (h % 2) * 256:(h % 2 + 1) * 256]
                P.stt("dve", on[:, h * 256:(h + 1) * 256], ob, ssg[:, h:h + 1], gnw[:], ALU.mult, ALU.mult)
                if h % 2 == 1:
                    yield
            for h in range(4):
                db = pf[1 + h // 2][:, (h % 2) * 256:(h % 2 + 1) * 256]
                P.mm(db, kend[:, h, :], vg[:, h * 256:(h + 1) * 256], start=True, stop=True)
            P.tt("pool", oa[:], on[:], sgo[:], ALU.mult)
            yield
            for h in range(4):
                db = pf[1 + h // 2][:, (h % 2) * 256:(h % 2 + 1) * 256]
                P.stt("dve", Sg[:, h, :], Sg[:, h, :], epos[:, h, 63:64], db, ALU.mult, ALU.add)
                if h % 2 == 1:
                    yield
            P.cp("act", Sgb[:], Sg[:])
            if cfg.last_chunk(c):
                P.dma("sp", ogla_d[l, s].rearrange("h d e -> d h e"), Sg[:])
            for k in range(8):
                P.tr(pb[0][:, 512 + k * 64:512 + (k + 1) * 64], oa[:, k * 128:(k + 1) * 128], ident_b[0:64, 0:64])
            yield
            P.cp("act", oaT[:, :, half:half + 64], pb[0][:, 512:1024].rearrange("p (k t) -> p k t", t=64))
            yield

        def ssd_chain(c, s, t0, half):
            ND = 13
            bcd = lambda v: v.unsqueeze(2).bc([128, ND, 64])
            bcp = lambda v: v.unsqueeze(2).bc([128, 24 - ND, 64])
            P.tt("dve", cacc[:, 0:ND, :], xbc[:, 0:ND, 3:67], bcd(cwT[:, 3, 0:ND]), ALU.mult)
            P.tt("pool", cacc2[:], xbc[:, ND:24, 3:67], bcp(cwT[:, 3, ND:24]), ALU.mult)
            yield
            P.tt("dve", cacc[:, 0:ND, :], cacc[:, 0:ND, :], bcd(cbT_[:, 0:ND]), ALU.add)
            P.tt("pool", cacc2[:], cacc2[:], bcp(cbT_[:, ND:24]), ALU.add)
            yield
            for i in range(3):
                P.tt("dve", dtmp[:], xbc[:, 0:ND, i:i + 64], bcd(cwT[:, i, 0:ND]), ALU.mult)
                P.tt("pool", ptmp[:], xbc[:, ND:24, i:i + 64], bcp(cwT[:, i, ND:24]), ALU.mult)
                yield
                P.tt("dve", cacc[:, 0:ND, :], cacc[:, 0:ND, :], dtmp[:], ALU.add)
                P.tt("pool", cacc2[:], cacc2[:], ptmp[:], ALU.add)
                yield
            P.act(xact[:, 0:ND, :], cacc[:, 0:ND, :], AF.Silu)
            P.act(xact[:, ND:24, :], cacc2[:], AF.Silu)
            P.tt("dve", dtA[:], dtv[:], Aneg[:], ALU.mult)
            yield
            P.mm(pf[3][0:64, 0:32], triU[:], dtA[:], start=True, stop=True)
            P.mm(pf[3][0:64, 32:64], Lst[:], dtA[:], start=True, stop=True)
            P.mm(pf[3][:, 64:96], ones_f[0:64, :], dtA[:], start=True, stop=True)
            if cfg.last_chunk(c):
                P.cp("dve", cvo[:], xbc[:, :, 64:67].rearrange("p a t -> p t a"))
                P.tr(pf[3][0:72, 128:256], cvo[:].rearrange("p t a -> p (t a)"), ident_f[:])
                P.cp("dve", cvs[:], pf[3][0:72, 128:256])
                P.dma("sp", oconv_d[l, s].rearrange("t (a p) -> (t a) p", p=128), cvs[:])
            yield
            P.act(expa[:], pf[3][0:64, 0:32], AF.Exp)
            P.act(wend[:], pf[3][0:64, 32:64], AF.Exp)
            P.act(decS[:], pf[3][:, 64:96], AF.Exp)
            P.tt("pool", rhsD[:], dtA[:].unsqueeze(2).bc([64, 32, 64]), triU[:].unsqueeze(1).bc([64, 32, 64]), ALU.mult)
            yield
            P.tt("dve", wend[:], wend[:], dtv[:], ALU.mult)
            for rnd in range(2):
                for a in range(8):
                    P.tr(pb[1][0:64, a * 128:(a + 1) * 128], xact[:, rnd * 8 + a, :], ident_b[:])
                yield
                P.cp("act", xs_tm[:, rnd * 1024:(rnd + 1) * 1024], pb[1][0:64, :])
                yield
            for a in range(4):
                P.tr(pb[1][0:64, a * 128:(a + 1) * 128], xact[:, 16 + a, :], ident_b[:])
            for g in range(4):
                P.mm(pf[3][0:64, 256 + g * 64:256 + (g + 1) * 64], xact[:, 16 + g, :], xact[:, 20 + g, :], start=True, stop=True)
            yield
            P.cp("act", B_tm[:], pb[1][0:64, 0:512])
            P.tt("dve", cbm[:], pf[3][0:64, 256:512].rearrange("p (g i) -> p g i", i=64),
                 triU[:].unsqueeze(1).bc([64, 4, 64]), ALU.mult)
            yield
            xs3 = xs_tm[:].rearrange("p (h q) -> p h q", q=64)
            P.tt("pool", xdt[:].rearrange("p (h q) -> p h q", q=64), xs3, dtv[:].unsqueeze(2).bc([64, 32, 64]), ALU.mult)
            yield
            P.tt("pool", xw[:].rearrange("p (h q) -> p h q", q=64), xs3, wend[:].unsqueeze(2).bc([64, 32, 64]), ALU.mult)
            yield
            P.tt("pool", y2.rearrange("p (h q) -> p h q", q=64), xs3, Dbc[:].unsqueeze(2).bc([64, 32, 64]), ALU.mult)
            rhsDf = rhsD[:].rearrange("p h i -> p (h i)")
            Ef = E_[:].rearrange("p h i -> p (h i)")
            E4 = E_[:].rearrange("p (g a) i -> p g a i", a=8)
            W4 = WT[:].rearrange("p (g a) i -> p g a i", a=8)
            def d_exp(g):
                P.mm(pf[3][0:64, :], Lst[:], rhsDf[:, g * 512:(g + 1) * 512], start=True, stop=True)
                P.act(Ef[:, g * 512:(g + 1) * 512], pf[3][0:64, :], AF.Exp)

            def wt(g):
                P.tt("dve", W4[:, g], E4[:, g], cbm[:, g, :].unsqueeze(1).bc([64, 8, 64]), ALU.mult)

            d_exp(0)
            yield
            wt(0)
            yield
            for g in range(4):
                X = pf[4]
                Y = pf[5]
                if g + 1 < 4:
                    d_exp(g + 1)
                for a in range(8):
                    h = g * 8 + a
                    P.mm(X[0:64, a * 64:(a + 1) * 64], WT[:, h, :], xdt[:, h * 64:(h + 1) * 64], start=True, stop=True)
                P.mm(Y[0:64, :], xact[:, 20 + g, :], Ssb[:, g * 512:(g + 1) * 512], start=True, stop=True)
                yield
                if g + 1 < 4:
                    wt(g + 1)
                    yield
                P.tt("dve", ytmp[:].rearrange("p (a q) -> p a q", q=64), Y[0:64, :].rearrange("p (a q) -> p a q", q=64),
                     expa[:, g * 8:(g + 1) * 8].unsqueeze(2).bc([64, 8, 64]), ALU.mult)
                yield
                P.tt("dve", yv[:, g * 512:(g + 1) * 512], ytmp[:], X[0:64, :], ALU.add)
                yield
            for g in range(4):
                P.mm(pf[4 + g % 2][:, :], B_tm[:, g * 128:(g + 1) * 128], xw[:, g * 512:(g + 1) * 512], start=True, stop=True)
                if g == 0:
                    P.tt("pool", Ss[:].rearrange("p (h q) -> p h q", q=64), Ss[:].rearrange("p (h q) -> p h q", q=64),
                         decS[:].unsqueeze(2).bc([128, 32, 64]), ALU.mult)
                yield
                P.tt("dve", Ss[:, g * 512:(g + 1) * 512], Ss[:, g * 512:(g + 1) * 512], pf[4 + g % 2][:, :], ALU.add)
                yield
            P.cp("act", Ssb[:], Ss[:])
            P.tt("pool", yv, yv, y2, ALU.add)
            yield
            P.act(sz, zt[:], AF.Silu)
            yield
            P.tt("dve", yv, yv, sz, ALU.mult)
            P.memset("pool", ssy[:], 0.0)
            yield
            for g in range(4):
                P.act(junk[:], yv[:, g * 512:(g + 1) * 512], AF.Square, accum=ssy[:, g:g + 1])
            yield
            P.ts("dve", ssy[:], ssy[:], 1.0 / 512, EPS, ALU.mult, ALU.add)
            P.tt("pool", ssy[:], ssy[:], neghalf[0:64, 0:1].bc([64, 4]), ALU.pow)
            yield
            for g in range(4):
                P.stt("dve", yn[:, g * 512:(g + 1) * 512], yv[:, g * 512:(g + 1) * 512],
                      ssy[:, g:g + 1], snw[:, g * 512:(g + 1) * 512], ALU.mult, ALU.mult)
                if g % 2 == 1:
                    yield
            for a in range(16):
                P.tr(pb[1][:, a * 64:(a + 1) * 64], yn[:, a * 128:(a + 1) * 128], ident_b[0:64, 0:64])
            yield
            P.cp("act", yT[:, :, half:half + 64], pb[1][:, :].rearrange("p (a t) -> p a t", t=64))
            if cfg.last_chunk(c):
                for a in range(16):
                    pk = pf[4 + a % 2]
                    P.tr(pk[:, 0:128], Ss[:, a * 128:(a + 1) * 128], ident_f[:])
                    evac(sst[:, a, :], pk[:, 0:128])
                P.dma("sp", ossd_d[l, s].rearrange("(a p) n -> p a n", p=128), sst)
            yield

        def chunk_loads(c):
            s = cfg.seq_of_chunk(c)
            t0 = c * 64
            first = cfg.first_chunk(c)
            P.dma("sp", qk[:], featq[:, :, t0:t0 + 64])
            P.dma("sp", gkl[:], gkl_d[:, t0:t0 + 64])
            P.dma("sp", vg[:], tokM_d[t0:t0 + 64, 0:2048])
            P.dma("sp", zt[:], tokM_d[t0:t0 + 64, 2048:4096])
            P.dma("sp", dtv[:], dt_d[t0:t0 + 64, :])
            if first:
                P.dma("sp", xbc[:, :, 3:67], featx[:, :, t0:t0 + 64])
                if s == 0:
                    P.memset("pool", xbc[:, :, 0:3], 0.0)
                    P.memset("pool", Sg[:], 0.0)
                    P.memset("pool", Sgb[:], 0.0)
                    P.memset("pool", Ss[:], 0.0)
                    P.memset("pool", Ssb[:], 0.0)
                else:
                    P.dma("sp", cst[:], sconv_d[l, s - 1].rearrange("t (a p) -> (t a) p", p=128))
                    P.tr(pf[4][:, 0:72], cst[:], ident_f[0:72, 0:72])
                    P.cp("dve", xbc[:, :, 0:3], pf[4][:, 0:72].rearrange("p (t a) -> p a t", a=24))
                    P.dma("sp", Sg[:], sgla_d[l, s - 1].rearrange("h d e -> d h e"))
                    P.cp("act", Sgb[:], Sg[:])
                    P.dma("sp", sst, sssd_d[l, s - 1].rearrange("(a p) n -> p a n", p=128))
                    for a in range(16):
                        pk = pf[a % 4]
                        P.tr(pk[:, 0:128], sst[:, a, :], ident_f[:])
                        evac(Ss[:, a * 128:(a + 1) * 128], pk[:, 0:128])
                    P.cp("act", Ssb[:], Ss[:])
            else:
                P.dma("sp", xbc[:], featx[:, :, t0 - 3:t0 + 64])

        for c in range(NCH):
            s = cfg.seq_of_chunk(c)
            t0 = c * 64
            half = (c % 2) * 64
            first = cfg.first_chunk(c)
            if c == 0:
                chunk_loads(0)
            ga = gla_chain(c, s, t0, half)
            sa = ssd_chain(c, s, t0, half)
            live = [sa, ga]
            while live:
                for gen in list(live):
                    try:
                        next(gen)
                    except StopIteration:
                        live.remove(gen)
            if c + 1 < NCH:
                chunk_loads(c + 1)
            if c % 2 == 1:
                tt0 = (c // 2) * 128
                P.dma("sp", gts[:], featg[:, :, tt0:tt0 + 128])
                P.act(gsg[:], gts[:], AF.Sigmoid)
                P.dma("sp", xt_[:], xT_d[:].rearrange("(k p) t -> p k t", p=128)[:, :, tt0:tt0 + 128])
                for n in range(8):
                    pa = pf[n // 4][:, (n % 4) * 128:(n % 4 + 1) * 128]
                    for k in range(8):
                        P.mm(pa, wgp[:, k, n * 128:(n + 1) * 128], oaT[:, k, :], start=(k == 0), stop=(k == 7))
                for n in range(8):
                    pbk = pf[2 + n // 4][:, (n % 4) * 128:(n % 4 + 1) * 128]
                    for k in range(16):
                        P.mm(pbk, wsp[:, k, n * 128:(n + 1) * 128], yT[:, k, :], start=(k == 0), stop=(k == 15))
                for hh in range(2):
                    P.tt("dve", bra[:, hh * 4:(hh + 1) * 4, :], pf[hh][:, :].rearrange("p (n t) -> p n t", t=128),
                         gsg[:, hh * 4:(hh + 1) * 4, :], ALU.mult)
                    P.tt("dve", gsg[:, 8 + hh * 4:12 + hh * 4, :], pf[2 + hh][:, :].rearrange("p (n t) -> p n t", t=128),
                         gsg[:, 8 + hh * 4:12 + hh * 4, :], ALU.mult)
                P.tt("dve", mT[:], bra[:], gsg[:, 8:16, :], ALU.add)
                for n in range(8):
                    py = pf[4 + n // 4][:, (n % 4) * 128:(n % 4 + 1) * 128]
                    for k in range(8):
                        P.mm(py, wop[:, k, n * 128:(n + 1) * 128], mT[:, k, :], start=(k == 0), stop=(k == 7))
                for n in range(8):
                    py = pf[4 + n // 4][:, (n % 4) * 128:(n % 4 + 1) * 128]
                    for (s0, ln, sq_) in cfg.segments(tt0, tt0 + 128):
                        P.stt("dve", xt_[:, n, s0 - tt0:s0 - tt0 + ln], py[:, s0 - tt0:s0 - tt0 + ln],
                              modG1[:, l, n, sq_:sq_ + 1], xt_[:, n, s0 - tt0:s0 - tt0 + ln], ALU.mult, ALU.add)
                P.dma("sp", xT_d[:].rearrange("(k p) t -> p k t", p=128)[:, :, tt0:tt0 + 128], xt_[:])
        P.pop_scope()

    def final_norm():
        P.push_scope()
        xg = [P.sb("fxg%d" % i, [128, 8, 128], F32) for i in range(2)]
        sq = P.sb("fsq", [128, 8, 128], F32)
        rstd = P.sb("frstd", [128, 128], F32)
        tmpn = P.sb("ftmp", [128, 8, 128], F32)
        yo = [P.sb("fyo%d" % i, [128, D], F32) for i in range(2)]
        for t in range(NTL):
            xv = xg[t % 2]
            P.dma("sp", xv[:], xT_d[:].rearrange("(k p) t -> p k t", p=128)[:, :, t * 128:(t + 1) * 128])
            P.act(sq[:], xv[:], AF.Square)
            for k in range(8):
                P.mm(pf[0][:, 0:128], ones_f[:], sq[:, k, :], start=(k == 0), stop=(k == 7))
            P.act(rstd[:], pf[0][:, 0:128], AF.Sqrt, bias=epsb[:, 0:1], scale=1.0 / D)
            P.op("dve", lambda E, o=rstd[:].ap: E.reciprocal(o, o), reads=[rstd], writes=[rstd])
            P.tt("dve", tmpn[:], xv[:], rstd[:].unsqueeze(1).bc([128, 8, 128]), ALU.mult)
            P.tt("dve", tmpn[:], tmpn[:], fnw[:].unsqueeze(2).bc([128, 8, 128]), ALU.mult)
            yv_ = yo[t % 2]
            for k in range(8):
                pk = pf[1 + k % 4]
                P.tr(pk[:, 0:128], tmpn[:, k, :], ident_f[:])
                evac(yv_[:, k * 128:(k + 1) * 128], pk[:, 0:128])
            P.dma("sp", y_d[t * 128:(t + 1) * 128, :], yv_[:])
        P.pop_scope()

    PEER = peer_builder(P, cfg, locals())

    for l in range(DEPTH):
        P.push_scope()
        hTbox[0] = P.sb("hT", [128, 8, NT], BF16)
        norm_mod(modA1, modB1, l)
        in_proj(l)
        P.pop_scope()
        mixers(l)
        if cfg.do_peer:
            P.push_scope()
            hTbox[0] = P.sb("hT2", [128, 8, NT], BF16)
            norm_mod(modA2, modB2, l)
            PEER(l)
            P.pop_scope()
    final_norm()
    P.finish(outs)
    P.run_block()
    P.close()
    return P


def peer_builder(P, cfg, env):
    NT, NTL = cfg.NT, cfg.NTL
    pf, pb = env["pf"], env["pb"]
    hTbox = env["hTbox"]
    ident_f, ident_b, iota_f = env["ident_f"], env["ident_b"], env["iota_f"]
    peer_wq, keys1, keys2, peer_u, peer_v = env["peer_wq"], env["keys1"], env["keys2"], env["peer_u"], env["peer_v"]
    uT_d, vb_d, xT_d, modG2 = env["uT_d"], env["vb_d"], env["xT_d"], env["modG2"]
    rr = [0]

    def evac(o, i):
        rr[0] += 1
        P.cp("act" if rr[0] % 2 else "dve", o, i)

    def PEER(l):
        hT = hTbox[0]
        P.push_scope()
        i1T = P.sb("i1T", [128, NT], BF16)
        i2T = P.sb("i2T", [128, NT], BF16)
        gT = P.sb("gT", [128, NT], BF16)
        iota_b = P.sb("iota_b", [128, 128], BF16)
        P.cp("dve", iota_b[:], iota_f[:])
        P.push_scope()
        ub = [P.sb("ub%d" % i, [128, 4, D], BF16) for i in range(2)]
        us = [P.sb("us%d" % i, [128, 4, 8, 128], BF16) for i in range(2)]
        vs = [P.sb("vs%d" % i, [128, 8, D], BF16) for i in range(2)]

        def pre_u(b):
            u_ = ub[b % 2]
            s_ = us[b % 2]
            P.dma("pool", u_[:], peer_u[l, :, b * 4:(b + 1) * 4, :])
            for ii in range(4):
                pk = pb[ii % 2]
                for k in range(8):
                    P.tr(pk[:, k * 128:(k + 1) * 128], u_[:, ii, k * 128:(k + 1) * 128], ident_b[:])
                P.cp("act", s_[:, ii, :, :], pk[:, :].rearrange("p (k i) -> p k i", i=128))
            P.dma("sp", uT_d[b * 4:(b + 1) * 4].rearrange("i p k j -> p i k j"), s_[:])

        def pre_v(b):
            v_ = vs[b % 2]
            P.dma("pool", v_[:], peer_v[l, :, b * 8:(b + 1) * 8, :])
            P.dma("sp", vb_d[:, b * 8:(b + 1) * 8, :], v_[:])

        pre_jobs = []
        for b in range(32):
            pre_jobs.append((pre_u, b))
            if b % 2 == 1:
                pre_jobs.append((pre_v, b // 2))
        per_tile = -(-len(pre_jobs) // NTL)
        wq = P.sb("wq", [128, 8, 2048], BF16)
        P.dma("pool", wq[:], peer_wq[l].rearrange("(k p) n -> p k n", p=128))
        kT = P.sb("kT", [128, 16, 128], BF16)
        ktmp = P.sb("ktmp", [128, 128], F32)
        for h in range(8):
            for wh, kd in enumerate((keys1, keys2)):
                P.dma("sp", ktmp[:], kd[l, h])
                P.tr(pf[4][:, 0:128], ktmp[:], ident_f[:])
                P.cp("dve", kT[:, h * 2 + wh, :], pf[4][:, 0:128])
        qT = P.sb("qT", [128, 16, 128], BF16)
        sc = P.sb("sc", [128, 16, 128], F32)
        wk = P.sb("wk", [128, 256], F32)
        v12 = P.sb("v12", [128, 16, 16], F32)
        x12 = P.sb("x12", [128, 16, 16], U32)
        x12f = P.sb("x12f", [128, 16, 16], F32)
        cand = P.sb("cand", [128, 8, 256], F32)
        s16 = P.sb("s16", [128, 8, 16], F32)
        p16 = P.sb("p16", [128, 8, 16], U32)
        pau = P.sb("pau", [128, 8, 16], U32)
        pbu = P.sb("pbu", [128, 8, 16], U32)
        af = P.sb("af", [128, 8, 16], F32)
        bf_ = P.sb("bf_", [128, 8, 16], F32)
        oh = P.sb("oh", [128, 8, 16, 16], F32)
        i1f = P.sb("i1f", [128, 8, 16], F32)
        i2f = P.sb("i2f", [128, 8, 16], F32)
        esum = P.sb("esum", [128, 8], F32)
        gsm = P.sb("gsm", [128, 8, 16], F32)
        for t in range(NTL):
            ts0 = t * 128
            for j in range(16):
                pk = pf[j % 4]
                for k in range(8):
                    P.mm(pk[:, 0:128], wq[:, k, j * 128:(j + 1) * 128], hT[:, k, ts0:ts0 + 128], start=(k == 0), stop=(k == 7))
                evac(qT[:, j, :], pk[:, 0:128])
            for j in range(16):
                pk = pf[j % 4]
                P.mm(pk[:, 0:128], qT[:, j, :], kT[:, j, :], start=True, stop=True)
                evac(sc[:, j, :], pk[:, 0:128])

            def top16(src, vals, idx, width):
                P.op("dve", lambda E, o=vals[:, 0:8].ap, i=src.ap: E.max(out=o, in_=i), reads=[src], writes=[vals])
                P.op("dve", lambda E, o=idx[:, 0:8].ap, m=vals[:, 0:8].ap, i=src.ap: E.max_index(out=o, in_max=m, in_values=i),
                     reads=[src, vals], writes=[idx])
                P.op("dve", lambda E, o=wk[:, 0:width].ap, m=vals[:, 0:8].ap, i=src.ap:
                     E.match_replace(out=o, in_to_replace=m, in_values=i, imm_value=-1e30), reads=[src, vals], writes=[wk])
                P.op("dve", lambda E, o=vals[:, 8:16].ap, i=wk[:, 0:width].ap: E.max(out=o, in_=i), reads=[wk], writes=[vals])
                P.op("dve", lambda E, o=idx[:, 8:16].ap, m=vals[:, 8:16].ap, i=wk[:, 0:width].ap:
                     E.max_index(out=o, in_max=m, in_values=i), reads=[wk, vals], writes=[idx])

            for _ in range(per_tile):
                if pre_jobs:
                    fn_, b_ = pre_jobs.pop(0)
                    fn_(b_)
            for j in range(16):
                top16(sc[:, j, :], v12[:, j, :], x12[:, j, :], 128)
            P.cp("dve", x12f[:], x12[:])
            v4 = v12[:].rearrange("p (h w) a -> p h w a", w=2)
            x4 = x12f[:].rearrange("p (h w) a -> p h w a", w=2)
            P.tt("dve", cand[:].rearrange("p h (a b) -> p h a b", b=16), v4[:, :, 0, :].unsqueeze(3).bc([128, 8, 16, 16]),
                 v4[:, :, 1, :].unsqueeze(2).bc([128, 8, 16, 16]), ALU.add)
            for h in range(8):
                top16(cand[:, h, :], s16[:, h, :], p16[:, h, :], 256)
            P.ts("dve", pau[:], p16[:], 4, None, ALU.logical_shift_right)
            P.ts("dve", pbu[:], p16[:], 15, None, ALU.bitwise_and)
            P.cp("dve", af[:], pau[:])
            P.cp("dve", bf_[:], pbu[:])
            io16 = iota_f[:, 0:16].unsqueeze(1).unsqueeze(1).bc([128, 8, 16, 16])
            for (sel, xi, dst) in ((af, 0, i1f), (bf_, 1, i2f)):
                P.tt("dve", oh[:], sel[:].unsqueeze(3).bc([128, 8, 16, 16]), io16, ALU.is_equal)
                P.tt("pool", oh[:], oh[:], x4[:, :, xi, :].unsqueeze(2).bc([128, 8, 16, 16]), ALU.mult)
                P.op("dve", lambda E, o=dst[:].ap, i=oh[:].ap: E.tensor_reduce(o, i, AX.X, ALU.add), reads=[oh], writes=[dst])
            P.tt("dve", gsm[:], s16[:], s16[:, :, 0:1].bc([128, 8, 16]), ALU.subtract)
            P.act(gsm[:], gsm[:], AF.Exp)
            P.op("dve", lambda E, o=esum[:].ap, i=gsm[:].ap: E.tensor_reduce(o, i, AX.X, ALU.add), reads=[gsm], writes=[esum])
            P.op("dve", lambda E, o=esum[:].ap: E.reciprocal(o, o), reads=[esum], writes=[esum])
            P.tt("dve", gsm[:], gsm[:], esum[:].unsqueeze(2).bc([128, 8, 16]), ALU.mult)
            for (src, dstT) in ((i1f, i1T), (i2f, i2T), (gsm, gT)):
                P.tr(pf[5][:, 0:128], src[:].rearrange("p h k -> p (h k)"), ident_f[:])
                P.cp("act", dstT[:, ts0:ts0 + 128], pf[5][:, 0:128])
        P.pop_scope()

        TG = 256
        G = P.sb("G", [128, 128, TG], BF16)
        A8 = [P.sb("A8_%d" % i, [128, 16, 128], BF16) for i in range(2)]
        B8 = [P.sb("B8_%d" % i, [128, 16, 128], BF16) for i in range(2)]
        IB = 4
        uTb = [P.sb("uTb%d" % i, [128, IB, 8, 128], BF16) for i in range(2)]
        vbb = [P.sb("vbb%d" % i, [128, IB, D], BF16) for i in range(2)]
        ge = [P.sb("ge%d" % i, [128, TG], BF16) for i in range(2)]
        gw = [P.sb("gw%d" % i, [128, TG], BF16) for i in range(2)]
        xg = P.sb("pxg", [128, 8, TG], F32)
        iob = iota_b[:].unsqueeze(1).bc([128, 16, 128])
        for g0 in range(0, NT, TG):
            n = min(TG, NT - g0)
            for q8 in range(n // 16):
                tk = g0 + q8 * 16
                a_ = A8[q8 % 2]
                b_ = B8[q8 % 2]
                P.tt("dve", a_[:], iob, i1T[:, tk:tk + 16].unsqueeze(2).bc([128, 16, 128]), ALU.is_equal)
                P.tt("dve", b_[:], iob, i2T[:, tk:tk + 16].unsqueeze(2).bc([128, 16, 128]), ALU.is_equal)
                P.tt("pool", b_[:], b_[:], gT[:, tk:tk + 16].unsqueeze(2).bc([128, 16, 128]), ALU.mult)
                for hh in range(4):
                    pk = pf[hh]
                    pk3 = pk[:, :].rearrange("p (i t) -> p i t", t=4)
                    for u in range(4):
                        P.mm(pk3[:, :, u], a_[:, hh * 4 + u, :], b_[:, hh * 4 + u, :], start=True, stop=True)
                    tq = q8 * 16 + hh * 4
                    P.cp("act", G[:, :, tq:tq + 4], pk3)
            def load_blk(b):
                P.dma("sp", uTb[b % 2][:], uT_d[b * IB:(b + 1) * IB].rearrange("i p k j -> p i k j"))
                P.dma("sp", vbb[b % 2][:], vb_d[:, b * IB:(b + 1) * IB, :])

            def act_mm(i2):
                ut = uTb[(i2 // IB) % 2]
                pa = pf[4 + i2 % 2]
                for k in range(8):
                    P.mm(pa[:, 0:n], ut[:, i2 % IB, k, :], hT[:, k, g0:g0 + n], start=(k == 0), stop=(k == 7))

            load_blk(0)
            act_mm(0)
            for i2 in range(128):
                if i2 % IB == 0 and i2 // IB + 1 < 128 // IB:
                    load_blk(i2 // IB + 1)
                if i2 + 1 < 128:
                    act_mm(i2 + 1)
                pa = pf[4 + i2 % 2]
                vt = vbb[(i2 // IB) % 2]
                ge_ = ge[i2 % 2]
                gw_ = gw[i2 % 2]
                P.act(ge_[:, 0:n], pa[:, 0:n], AF.Gelu)
                P.tt("dve", gw_[:, 0:n], ge_[:, 0:n], G[:, i2, 0:n], ALU.mult)
                for j in range(8):
                    P.mm(pf[j // 2][:, (j % 2) * 256:(j % 2) * 256 + n], vt[:, i2 % IB, j * 128:(j + 1) * 128], gw_[:, 0:n],
                         start=(i2 == 0), stop=(i2 == 127))
            P.dma("sp", xg[:, :, 0:n], xT_d[:].rearrange("(k p) t -> p k t", p=128)[:, :, g0:g0 + n])
            for j in range(8):
                for (s0, ln, sq_) in cfg.segments(g0, g0 + n):
                    o0 = s0 - g0
                    P.stt("dve", xg[:, j, o0:o0 + ln], pf[j // 2][:, (j % 2) * 256 + o0:(j % 2) * 256 + o0 + ln],
                          modG2[:, l, j, sq_:sq_ + 1], xg[:, j, o0:o0 + ln], ALU.mult, ALU.add)
            P.dma("sp", xT_d[:].rearrange("(k p) t -> p k t", p=128)[:, :, g0:g0 + n], xg[:, :, 0:n])
        P.pop_scope()

    return PEER


from concourse.bass_utils import run_bass_kernel_spmd

_WNAMES = ["w_ada", "b_ada", "norm1_w", "w_in", "gla_gk_w2", "gla_gk_b", "gla_norm_w", "gla_proj", "ssd_conv_w",
           "ssd_conv_b", "ssd_dt_bias", "ssd_A_log", "ssd_D", "ssd_norm_w", "ssd_proj", "w_out", "norm2_w", "peer_wq",
           "peer_keys1", "peer_keys2", "peer_u", "peer_v", "final_norm_w"]


def _prep_weights(inp, depth):
    f = lambda a: np.ascontiguousarray(np.asarray(a, dtype=np.float32))
    w = {k: f(inp[k]) for k in _WNAMES}
    w["b_ada"] = w["b_ada"].reshape(depth, 48, 128)
    w["norm1_w"] = w["norm1_w"].reshape(depth, 8, 128)
    w["norm2_w"] = w["norm2_w"].reshape(depth, 8, 128)
    w["gla_gk_b"] = w["gla_gk_b"].reshape(depth, 4, 128)
    w["ssd_conv_w"] = w["ssd_conv_w"].reshape(depth, 4, 24, 128)
    w["ssd_conv_b"] = w["ssd_conv_b"].reshape(depth, 24, 128)
    w["peer_u"] = w["peer_u"].reshape(depth, 128, 128, 1024)
    w["peer_v"] = w["peer_v"].reshape(depth, 128, 128, 1024)
    w["final_norm_w"] = w["final_norm_w"].reshape(8, 128)
    return w


def run_cores(inp, n_cores, TP, NS, depth, do_peer=True):
    cfg = Cfg(TP, NS, depth, do_peer)
    nc = bass.Bass("TRN2", target_bir_lowering=False)
    build(nc, cfg)
    w = _prep_weights(inp, depth)
    f = lambda a: np.ascontiguousarray(np.asarray(a, dtype=np.float32))
    xp, xs = f(inp["x_prompt"]), f(inp["x_sample"])
    cp, cs = f(inp["c_prompt"]), f(inp["c_sample"])
    sg, ss, scv = f(inp["state_gla"]), f(inp["state_ssd"]), f(inp["state_conv"])
    in_maps = []
    for i in range(n_cores):
        m = dict(w)
        m["x"] = np.concatenate([xp[i], xs[i * NS:(i + 1) * NS].reshape(NS * 64, D)], axis=0)
        m["cvec"] = np.concatenate([cp[i:i + 1], cs[i * NS:(i + 1) * NS]], axis=0)
        m["sgla"] = np.ascontiguousarray(sg[:, i * NS:(i + 1) * NS])
        m["sssd"] = np.ascontiguousarray(ss[:, i * NS:(i + 1) * NS]).reshape(depth, NS, 2048, 128)
        m["sconv"] = np.ascontiguousarray(scv[:, i * NS:(i + 1) * NS])
        in_maps.append(m)
    res = run_bass_kernel_spmd(nc, in_maps, core_ids=list(range(n_cores)))
    B, BS = n_cores, n_cores * NS
    y_p = np.zeros((B, TP, D), np.float32)
    y_s = np.zeros((BS, 64, D), np.float32)
    gla_p = np.zeros((depth, B, 4, 128, 256), np.float32)
    ssd_p = np.zeros((depth, B, 32, 64, 128), np.float32)
    conv_p = np.zeros((depth, B, 3, 3072), np.float32)
    gla_s = np.zeros((depth, BS, 4, 128, 256), np.float32)
    ssd_s = np.zeros((depth, BS, 32, 64, 128), np.float32)
    conv_s = np.zeros((depth, BS, 3, 3072), np.float32)
    for i in range(n_cores):
        r = res.results[i]
        y = np.asarray(r["y"])
        y_p[i] = y[:TP]
        y_s[i * NS:(i + 1) * NS] = y[TP:].reshape(NS, 64, D)
        og = np.asarray(r["ogla"])
        os_ = np.asarray(r["ossd"]).reshape(depth, 1 + NS, 32, 64, 128)
        oc = np.asarray(r["oconv"])
        gla_p[:, i] = og[:, 0]
        ssd_p[:, i] = os_[:, 0]
        conv_p[:, i] = oc[:, 0]
        gla_s[:, i * NS:(i + 1) * NS] = og[:, 1:]
        ssd_s[:, i * NS:(i + 1) * NS] = os_[:, 1:]
        conv_s[:, i * NS:(i + 1) * NS] = oc[:, 1:]
    return (y_p, y_s, gla_p, ssd_p, conv_p, gla_s, ssd_s, conv_s)


def kernel(**inputs):
    return run_cores(inputs, 8, 2048, 4, 2)
```

```python
from contextlib import ExitStack
import numpy as np
import concourse.bass as bass
import concourse.mybir as mybir

F32 = mybir.dt.float32
BF16 = mybir.dt.bfloat16
I32 = mybir.dt.int32
U32 = mybir.dt.uint32
AF = mybir.ActivationFunctionType
ALU = mybir.AluOpType
AX = mybir.AxisListType

ENGS = ("pe", "act", "dve", "pool", "sp")
SKIP_SAME = ("pe", "act")


class V:
    __slots__ = ("ap", "buf")

    def __init__(self, ap, buf):
        self.ap = ap
        self.buf = buf

    def __getitem__(self, k):
        return V(self.ap[k], self.buf)

    def rearrange(self, pat_, **kw):
        return V(self.ap.rearrange(pat_, **kw), self.buf)

    def unsqueeze(self, a):
        return V(self.ap.unsqueeze(a), self.buf)

    def bc(self, shape):
        return V(self.ap.to_broadcast(list(shape)), self.buf)

    def pbc(self, n):
        return V(self.ap.partition_broadcast(n), self.buf)

    def bitcast(self, dt):
        return V(self.ap.bitcast(dt), self.buf)


class Buf:
    __slots__ = ("t", "name", "w", "r")

    def __init__(self, t, name):
        self.t = t
        self.name = name
        self.w = None
        self.r = []

    def __getitem__(self, k):
        return V(self.t[k], self)


def _aps(x):
    return x.ap if isinstance(x, V) else x


class Prog:
    def __init__(self, nc, n_dma_sems=24):
        self.nc = nc
        self.es = ExitStack()
        self.thunks = {e: [] for e in ENGS}
        self.sems = {}
        self.cnt = {}
        for e in ENGS:
            self.sems[e] = self.es.enter_context(nc.semaphore("s_" + e))
            self.cnt[e] = 0
        self.dq = {}
        for q, n in (("sp", n_dma_sems), ("pool", n_dma_sems)):
            lst = []
            for i in range(n):
                key = "d_%s_%d" % (q, i)
                self.sems[key] = self.es.enter_context(nc.semaphore(key))
                self.cnt[key] = 0
                lst.append(key)
            self.dq[q] = [lst, 0]
        self.seen = {e: {} for e in ENGS}
        self.n_instr = 0
        self.n_wait = 0
        self.scopes = []

    def _stack(self):
        return self.scopes[-1] if self.scopes else self.es

    def sb(self, name, shape, dtype):
        self.uid = getattr(self, "uid", 0) + 1
        name = "%s_%d" % (name, self.uid)
        t = self._stack().enter_context(self.nc.sbuf_tensor(name, list(shape), dtype))
        return Buf(t, name)

    def ps(self, name, shape, dtype=F32):
        t = self._stack().enter_context(self.nc.psum_tensor(name, list(shape), dtype))
        return Buf(t, name)

    def dram(self, name, shape, dtype, kind="Internal"):
        t = self.nc.dram_tensor(name, list(shape), dtype, kind=kind)
        return Buf(t.ap(), name)

    def push_scope(self):
        self.scopes.append(ExitStack())

    def pop_scope(self):
        self.barrier()
        self.scopes.pop().close()

    def barrier(self):
        for e in ENGS:
            waits = {}
            for key, val in self.cnt.items():
                if val > 0 and key != e and self.seen[e].get(key, 0) < val:
                    waits[key] = val
            self._emit_waits(e, waits)

    def _need(self, eng, ev, waits):
        if ev is None:
            return
        key, val, src = ev
        if src == eng and eng in SKIP_SAME:
            return
        if self.seen[eng].get(key, 0) >= val:
            return
        if waits.get(key, 0) < val:
            waits[key] = val

    def _deps(self, eng, reads, writes):
        waits = {}
        for b in reads:
            self._need(eng, b.w, waits)
        for b in writes:
            self._need(eng, b.w, waits)
            for ev in b.r:
                self._need(eng, ev, waits)
        return waits

    def _emit_waits(self, eng, waits):
        for key, val in waits.items():
            sem = self.sems[key]
            self.thunks[eng].append(lambda E, sem=sem, val=val: E.wait_ge(sem, val))
            self.seen[eng][key] = val
            self.n_wait += 1

    def _record(self, ev, reads, writes):
        for b in writes:
            b.w = ev
            b.r = []
        for b in reads:
            if b.w is not ev:
                b.r.append(ev)
                if len(b.r) > 16:
                    best = {}
                    for k, v, s in b.r:
                        if best.get(k, (0, None))[0] < v:
                            best[k] = (v, s)
                    b.r = [(k, v, s) for k, (v, s) in best.items()]

    def op(self, eng, fn, reads=(), writes=()):
        reads = [x.buf if isinstance(x, V) else x for x in reads]
        writes = [x.buf if isinstance(x, V) else x for x in writes]
        waits = self._deps(eng, reads, writes)
        self._emit_waits(eng, waits)
        self.cnt[eng] += 1
        val = self.cnt[eng]
        sem = self.sems[eng]
        self.thunks[eng].append(lambda E, fn=fn, sem=sem: fn(E).then_inc(sem, 1))
        self._record((eng, val, eng), reads, writes)
        self.n_instr += 1

    def dma(self, q, out, in_, **kw):
        reads = [in_.buf]
        writes = [out.buf]
        waits = self._deps(q, reads, writes)
        lst, idx = self.dq[q]
        key = lst[idx % len(lst)]
        self.dq[q][1] = idx + 1
        if self.cnt[key] > 0 and self.seen[q].get(key, 0) < self.cnt[key]:
            if waits.get(key, 0) < self.cnt[key]:
                waits[key] = self.cnt[key]
        self._emit_waits(q, waits)
        self.cnt[key] += 16
        val = self.cnt[key]
        sem = self.sems[key]
        self.thunks[q].append(
            lambda E, o=out.ap, i=in_.ap, sem=sem, kw=kw: E.dma_start(out=o, in_=i, **kw).then_inc(sem, 16))
        self._record((key, val, "dma_" + q), reads, writes)
        self.n_instr += 1

    def mm(self, o, l, r, start=True, stop=True):
        self.op("pe", lambda E, o=o.ap, l=l.ap, r=r.ap: E.matmul(o, l, r, start=start, stop=stop),
                reads=[l, r], writes=[o])

    def tr(self, o, i, ident):
        self.op("pe", lambda E, o=o.ap, i=i.ap, d=ident.ap: E.transpose(o, i, d), reads=[i, ident], writes=[o])

    def act(self, o, i, func, bias=None, scale=None, accum=None, eng="act"):
        rd = [i]
        kw = {}
        if bias is not None:
            kw["bias"] = _aps(bias)
            if isinstance(bias, V):
                rd.append(bias)
        if scale is not None:
            kw["scale"] = _aps(scale)
            if isinstance(scale, V):
                rd.append(scale)
        wr = [o]
        if accum is not None:
            kw["accum_out"] = accum.ap
            wr.append(accum)
        self.op(eng, lambda E, o=o.ap, i=i.ap, kw=kw: E.activation(o, i, func, **kw), reads=rd, writes=wr)

    def tt(self, eng, o, a, b, op):
        self.op(eng, lambda E, o=o.ap, a=a.ap, b=b.ap: E.tensor_tensor(o, a, b, op), reads=[a, b], writes=[o])

    def ts(self, eng, o, a, s1, s2, op0, op1=None, accum=None):
        rd = [a] + [s for s in (s1, s2) if isinstance(s, V)]
        kw = {}
        wr = [o]
        if op1 is not None:
            kw["op1"] = op1
        if accum is not None:
            kw["accum_out"] = accum.ap
            wr.append(accum)
        self.op(eng, lambda E, o=o.ap, a=a.ap, s1=_aps(s1), s2=_aps(s2), kw=kw:
                E.tensor_scalar(o, a, s1, s2, op0, **kw), reads=rd, writes=wr)

    def stt(self, eng, o, a, s, b, op0, op1, accum=None):
        rd = [a, b] + ([s] if isinstance(s, V) else [])
        kw = {}
        wr = [o]
        if accum is not None:
            kw["accum_out"] = accum.ap
            wr.append(accum)
        self.op(eng, lambda E, o=o.ap, a=a.ap, s=_aps(s), b=b.ap, kw=kw:
                E.scalar_tensor_tensor(o, a, s, b, op0, op1, **kw), reads=rd, writes=wr)

    def cp(self, eng, o, i):
        if eng == "act":
            self.op(eng, lambda E, o=o.ap, i=i.ap: E.copy(o, i), reads=[i], writes=[o])
        else:
            self.op(eng, lambda E, o=o.ap, i=i.ap: E.tensor_copy(o, i), reads=[i], writes=[o])

    def memset(self, eng, o, val):
        self.op(eng, lambda E, o=o.ap: E.memset(o, val), writes=[o])

    def finish(self, out_bufs):
        waits = {}
        for b in out_bufs:
            self._need("sp", b.w, waits)
        self._emit_waits("sp", waits)

    def run_block(self):
        nc = self.nc
        names = {"pe": "tensor", "act": "scalar", "dve": "vector", "pool": "gpsimd", "sp": "sync"}
        with nc.Block() as block:
            for e in ENGS:
                th = self.thunks[e]

                def body(E, th=th):
                    for f in th:
                        f(E)
                getattr(block, names[e])(body)

    def close(self):
        while self.scopes:
            self.scopes.pop().close()
        self.es.close()
D = 1024
EPS = 1e-6
OFF_Q, OFF_K, OFF_V, OFF_GO, OFF_GKL, OFF_Z, OFF_XBC, OFF_DT, OFF_GATE = 0, 512, 1024, 2048, 3072, 3088, 5136, 8208, 8240
DIN = 10288


class Cfg:
    def __init__(self, TP, NS, DEPTH, do_peer=True):
        self.TP, self.NS, self.DEPTH = TP, NS, DEPTH
        self.NSEQ = 1 + NS
        self.NT = TP + 64 * NS
        assert self.NT % 128 == 0 and TP % 128 == 0
        self.NCH = self.NT // 64
        self.NTL = self.NT // 128
        self.do_peer = do_peer

    def seq_of_chunk(self, c):
        return 0 if c < self.TP // 64 else 1 + (c - self.TP // 64)

    def first_chunk(self, c):
        return c == 0 or c >= self.TP // 64

    def last_chunk(self, c):
        return c == self.TP // 64 - 1 or c >= self.TP // 64

    def segments(self, t0, t1):
        out = []
        t = t0
        while t < t1:
            if t < self.TP:
                e = min(t1, self.TP)
                out.append((t, e - t, 0))
            else:
                s = 1 + (t - self.TP) // 64
                e = min(t1, self.TP + (s) * 64)
                out.append((t, e - t, s))
            t = e
        return out


def build(nc, cfg):
    P = Prog(nc)
    NT, NSEQ, NS, NCH, NTL, DEPTH = cfg.NT, cfg.NSEQ, cfg.NS, cfg.NCH, cfg.NTL, cfg.DEPTH
    ein = lambda n, s, dt=F32: P.dram(n, s, dt, kind="ExternalInput")
    eout = lambda n, s, dt=F32: P.dram(n, s, dt, kind="ExternalOutput")
    x_d = ein("x", [NT, D])
    c_d = ein("cvec", [NSEQ, D])
    sgla_d = ein("sgla", [DEPTH, NS, 4, 128, 256])
    sssd_d = ein("sssd", [DEPTH, NS, 2048, 128])
    sconv_d = ein("sconv", [DEPTH, NS, 3, 3072])
    w_ada = ein("w_ada", [DEPTH, D, 6 * D])
    b_ada = ein("b_ada", [DEPTH, 48, 128])
    norm1_w = ein("norm1_w", [DEPTH, 8, 128])
    w_in = ein("w_in", [DEPTH, D, DIN])
    gk_w2 = ein("gla_gk_w2", [DEPTH, 16, 512])
    gk_b = ein("gla_gk_b", [DEPTH, 4, 128])
    gla_norm_w = ein("gla_norm_w", [DEPTH, 256])
    gla_proj = ein("gla_proj", [DEPTH, D, D])
    conv_w = ein("ssd_conv_w", [DEPTH, 4, 24, 128])
    conv_b = ein("ssd_conv_b", [DEPTH, 24, 128])
    dt_bias = ein("ssd_dt_bias", [DEPTH, 32])
    A_log = ein("ssd_A_log", [DEPTH, 32])
    ssd_D = ein("ssd_D", [DEPTH, 32])
    ssd_norm_w = ein("ssd_norm_w", [DEPTH, 2048])
    ssd_proj = ein("ssd_proj", [DEPTH, 2048, D])
    w_out = ein("w_out", [DEPTH, D, D])
    norm2_w = ein("norm2_w", [DEPTH, 8, 128])
    peer_wq = ein("peer_wq", [DEPTH, D, 2048])
    keys1 = ein("peer_keys1", [DEPTH, 8, 128, 128])
    keys2 = ein("peer_keys2", [DEPTH, 8, 128, 128])
    peer_u = ein("peer_u", [DEPTH, 128, 128, D])
    peer_v = ein("peer_v", [DEPTH, 128, 128, D])
    fnorm_w = ein("final_norm_w", [8, 128])
    y_d = eout("y", [NT, D])
    ogla_d = eout("ogla", [DEPTH, NSEQ, 4, 128, 256])
    ossd_d = eout("ossd", [DEPTH, NSEQ, 2048, 128])
    oconv_d = eout("oconv", [DEPTH, NSEQ, 3, 3072])
    outs = [y_d, ogla_d, ossd_d, oconv_d]
    xT_d = P.dram("xT_s", [D, NT], F32)
    featT_d = P.dram("featT_s", [6144, NT], BF16)
    gkl_d = P.dram("gkl_s", [16, NT], F32)
    tokM_d = P.dram("tokM_s", [NT, 4096], BF16)
    dt_d = P.dram("dt_s", [NT, 32], F32)
    uT_d = P.dram("uT_s", [128, 128, 8, 128], BF16)
    vb_d = P.dram("vb_s", [128, 128, D], BF16)

    ident_f = P.sb("ident_f", [128, 128], F32)
    ident_b = P.sb("ident_b", [128, 128], BF16)
    ones_f = P.sb("ones_f", [128, 128], F32)
    triU = P.sb("triU", [64, 64], F32)
    Lst = P.sb("Lst", [64, 64], F32)
    iota_f = P.sb("iota_f", [128, 128], F32)
    epsb = P.sb("epsb", [128, 1], F32)
    modA1 = P.sb("modA1", [128, DEPTH, 8, NSEQ], F32)
    modB1 = P.sb("modB1", [128, DEPTH, 8, NSEQ], F32)
    modG1 = P.sb("modG1", [128, DEPTH, 8, NSEQ], F32)
    modA2 = P.sb("modA2", [128, DEPTH, 8, NSEQ], F32)
    modB2 = P.sb("modB2", [128, DEPTH, 8, NSEQ], F32)
    modG2 = P.sb("modG2", [128, DEPTH, 8, NSEQ], F32)
    fnw = P.sb("fnw", [128, 8], F32)
    zeroc = P.sb("zeroc", [128, 1], F32)
    negsix = P.sb("negsix", [128, 1], F32)
    neghalf = P.sb("neghalf", [128, 1], F32)
    zer64 = P.sb("zer64", [128, 64], F32)
    hTbox = [None]
    pf = [P.ps("pf%d" % i, [128, 512], F32) for i in range(6)]
    pb = [P.ps("pb%d" % i, [128, 1024], BF16) for i in range(2)]

    P.memset("pool", ident_f[:], 0.0)
    P.op("pool", lambda E: E.affine_select(ident_f[:].ap, ident_f[:].ap, pattern=[[-1, 128]], compare_op=ALU.not_equal,
                                           fill=1.0, base=0, channel_multiplier=1), reads=[ident_f], writes=[ident_f])
    P.cp("dve", ident_b[:], ident_f[:])
    P.memset("pool", ones_f[:], 1.0)
    P.memset("pool", triU[:], 1.0)
    P.op("pool", lambda E: E.affine_select(triU[:].ap, triU[:].ap, pattern=[[1, 64]], compare_op=ALU.is_ge,
                                           fill=0.0, base=0, channel_multiplier=-1), reads=[triU], writes=[triU])
    P.memset("pool", Lst[:], 1.0)
    P.op("pool", lambda E: E.affine_select(Lst[:].ap, Lst[:].ap, pattern=[[-1, 64]], compare_op=ALU.is_ge,
                                           fill=0.0, base=-1, channel_multiplier=1), reads=[Lst], writes=[Lst])
    P.op("pool", lambda E: E.iota(iota_f[:].ap, pattern=[[1, 128]], base=0, channel_multiplier=0,
                                  allow_small_or_imprecise_dtypes=True), writes=[iota_f])
    P.memset("pool", epsb[:], EPS)
    P.memset("pool", zeroc[:], 0.0)
    P.memset("pool", negsix[:], -1.0 / 16.0)
    P.memset("pool", neghalf[:], -0.5)
    P.memset("pool", zer64[:], 0.0)

    rr = [0]

    def evac(o, i):
        rr[0] += 1
        P.cp("act" if rr[0] % 2 else "dve", o, i)

    def loadT(dst, src, n, tmp, pbank):
        P.dma("sp", tmp[0:n, 0:128], src)
        P.tr(pbank[:, 0:n], tmp[0:n, 0:128], ident_f[0:n, 0:n])
        P.cp("dve", dst, pbank[:, 0:n])

    P.push_scope()
    xin = [P.sb("xin%d" % i, [128, D], F32) for i in range(2)]
    xTs = [P.sb("xTs%d" % i, [128, 8, 128], F32) for i in range(2)]
    for t in range(NTL):
        xi = xin[t % 2]
        xo = xTs[t % 2]
        P.dma("sp", xi[:], x_d[t * 128:(t + 1) * 128, :])
        for k in range(8):
            pk = pf[k % 4]
            P.tr(pk[:, 0:128], xi[:, k * 128:(k + 1) * 128], ident_f[:])
            evac(xo[:, k, :], pk[:, 0:128])
        P.dma("sp", xT_d[:].rearrange("(k p) t -> p k t", p=128)[:, :, t * 128:(t + 1) * 128], xo[:])
    P.pop_scope()

    P.push_scope()
    tmpT = P.sb("tmpT", [128, 128], F32)
    cT = P.sb("cT", [128, 8, NSEQ], F32)
    siluT = P.sb("siluT", [128, 8, NSEQ], F32)
    badaT = P.sb("badaT", [128, 48], F32)
    n1T = P.sb("n1T", [128, 8], F32)
    n2T = P.sb("n2T", [128, 8], F32)
    modT = P.sb("modT", [128, 48, NSEQ], F32)
    wada = [P.sb("wada%d" % i, [128, 8, 768], F32) for i in range(2)]
    for k in range(8):
        loadT(cT[:, k, :], c_d[:, k * 128:(k + 1) * 128], NSEQ, tmpT, pf[4])
    P.act(siluT[:], cT[:], AF.Silu)
    loadT(fnw[:], fnorm_w[:, :], 8, tmpT, pf[4])
    for l in range(DEPTH):
        loadT(badaT[:], b_ada[l], 48, tmpT, pf[4])
        loadT(n1T[:], norm1_w[l], 8, tmpT, pf[4])
        loadT(n2T[:], norm2_w[l], 8, tmpT, pf[4])
        for blk in range(8):
            wb_ = wada[blk % 2]
            P.dma("sp", wb_[:], w_ada[l].rearrange("(k p) n -> p k n", p=128)[:, :, blk * 768:(blk + 1) * 768])
            for jj in range(6):
                j = blk * 6 + jj
                for k in range(8):
                    P.mm(pf[5][:, j * NSEQ:(j + 1) * NSEQ], wb_[:, k, jj * 128:(jj + 1) * 128], siluT[:, k, :],
                         start=(k == 0), stop=(k == 7))
        P.tt("dve", modT[:], pf[5][:, 0:48 * NSEQ].rearrange("p (j s) -> p j s", s=NSEQ),
             badaT[:].unsqueeze(2).bc([128, 48, NSEQ]), ALU.add)
        P.stt("dve", modA1[:, l], modT[:, 8:16, :], 1.0, n1T[:].unsqueeze(2).bc([128, 8, NSEQ]), ALU.add, ALU.mult)
        P.cp("dve", modB1[:, l], modT[:, 0:8, :])
        P.cp("dve", modG1[:, l], modT[:, 16:24, :])
        P.stt("dve", modA2[:, l], modT[:, 32:40, :], 1.0, n2T[:].unsqueeze(2).bc([128, 8, NSEQ]), ALU.add, ALU.mult)
        P.cp("dve", modB2[:, l], modT[:, 24:32, :])
        P.cp("dve", modG2[:, l], modT[:, 40:48, :])
    P.pop_scope()

    def norm_mod(A, B, l):
        P.push_scope()
        xg = [P.sb("xg%d" % i, [128, 8, 512], F32) for i in range(2)]
        sq = P.sb("sq", [128, 8, 512], F32)
        rstd = P.sb("rstd", [128, 512], F32)
        tmpn = [P.sb("tmpn%d" % i, [128, 512], F32) for i in range(2)]
        gi = 0
        for t0 in range(0, NT, 512):
            n = min(512, NT - t0)
            xv = xg[gi % 2]
            gi += 1
            P.dma("sp", xv[:, :, 0:n], xT_d[:].rearrange("(k p) t -> p k t", p=128)[:, :, t0:t0 + n])
            P.act(sq[:, :, 0:n], xv[:, :, 0:n], AF.Square)
            for k in range(8):
                P.mm(pf[0][:, 0:n], ones_f[:], sq[:, k, 0:n], start=(k == 0), stop=(k == 7))
            P.act(rstd[:, 0:n], pf[0][:, 0:n], AF.Sqrt, bias=epsb[:, 0:1], scale=1.0 / D)
            P.op("dve", lambda E, o=rstd[:, 0:n].ap: E.reciprocal(o, o), reads=[rstd], writes=[rstd])
            for k in range(8):
                tm = tmpn[k % 2]
                P.tt("dve", tm[:, 0:n], xv[:, k, 0:n], rstd[:, 0:n], ALU.mult)
                for (s0, ln, sq_) in cfg.segments(t0, t0 + n):
                    P.act(hTbox[0][:, k, s0:s0 + ln], tm[:, s0 - t0:s0 - t0 + ln], AF.Identity,
                          bias=B[:, l, k, sq_:sq_ + 1], scale=A[:, l, k, sq_:sq_ + 1])
        P.pop_scope()

    def in_proj(l):
        hT = hTbox[0]
        P.push_scope()
        wblk = [P.sb("wblk%d" % i, [128, 8, 512], BF16) for i in range(2)]
        wsm = P.sb("wsm", [128, 8, 48], BF16)
        stg = [P.sb("stg%d" % i, [128, NT], BF16) for i in range(2)]
        stgM = P.sb("stgM", [128, NTL, 512], BF16)
        stgk = P.sb("stgk", [16, NT], F32)
        stgd = P.sb("stgd", [128, NTL, 32], F32)
        wl = w_in[l].rearrange("(k p) n -> p k n", p=128)
        fblocks = [(OFF_Q, 0), (OFF_K, 512)] + [(OFF_XBC + 512 * i, 1024 + 512 * i) for i in range(6)] + \
                  [(OFF_GATE + 512 * i, 4096 + 512 * i) for i in range(4)]
        bi = 0
        pi = 0
        si = 0
        for (c0, r0) in fblocks:
            wv = wblk[bi % 2]
            bi += 1
            P.dma("pool", wv[:], wl[:, :, c0:c0 + 512])
            for j in range(4):
                st = stg[si % 2]
                si += 1
                for t0 in range(0, NT, 512):
                    n = min(512, NT - t0)
                    pk = pf[pi % 6]
                    pi += 1
                    for k in range(8):
                        P.mm(pk[:, 0:n], wv[:, k, j * 128:(j + 1) * 128], hT[:, k, t0:t0 + n], start=(k == 0), stop=(k == 7))
                    evac(st[:, t0:t0 + n], pk[:, 0:n])
                P.dma("sp", featT_d[r0 + j * 128:r0 + (j + 1) * 128, :], st[:])
        P.dma("pool", wsm[:, :, 0:16], wl[:, :, OFF_GKL:OFF_GKL + 16])
        P.dma("pool", wsm[:, :, 16:48], wl[:, :, OFF_DT:OFF_DT + 32])
        for t0 in range(0, NT, 512):
            n = min(512, NT - t0)
            pk = pf[pi % 6]
            pi += 1
            for k in range(8):
                P.mm(pk[0:16, 0:n], wsm[:, k, 0:16], hT[:, k, t0:t0 + n], start=(k == 0), stop=(k == 7))
            evac(stgk[:, t0:t0 + n], pk[0:16, 0:n])
        P.dma("sp", gkl_d[:, :], stgk[:])
        for t in range(NTL):
            pk = pf[pi % 6]
            pi += 1
            for k in range(8):
                P.mm(pk[:, 0:32], hT[:, k, t * 128:(t + 1) * 128], wsm[:, k, 16:48], start=(k == 0), stop=(k == 7))
            evac(stgd[:, t, :], pk[:, 0:32])
        dtb128 = P.sb("dtb128", [128, 32], F32)
        P.dma("sp", dtb128[:], dt_bias[l].pbc(128))
        P.tt("dve", stgd[:], stgd[:], dtb128[:].unsqueeze(1).bc([128, NTL, 32]), ALU.add)
        P.act(stgd[:], stgd[:], AF.Exp)
        P.act(stgd[:], stgd[:], AF.Ln, bias=ones_f[:, 0:1], scale=1.0)
        P.dma("sp", dt_d[:].rearrange("(t p) c -> p t c", p=128), stgd[:])
        mblocks = [(OFF_V + 512 * i, 512 * i) for i in range(2)] + [(OFF_GO + 512 * i, 1024 + 512 * i) for i in range(2)] + \
                  [(OFF_Z + 512 * i, 2048 + 512 * i) for i in range(4)]
        for (c0, m0) in mblocks:
            wv = wblk[bi % 2]
            bi += 1
            P.dma("pool", wv[:], wl[:, :, c0:c0 + 512])
            for t in range(NTL):
                pk = pf[pi % 6]
                pi += 1
                for k in range(8):
                    P.mm(pk[:, :], hT[:, k, t * 128:(t + 1) * 128], wv[:, k, :], start=(k == 0), stop=(k == 7))
                evac(stgM[:, t, :], pk[:, :])
            P.dma("sp", tokM_d[:].rearrange("(t p) c -> p t c", p=128)[:, :, m0:m0 + 512], stgM[:])
        P.pop_scope()

    def mixers(l):
        P.push_scope()
        tmpT = P.sb("tmpT2", [128, 128], F32)
        wgp = P.sb("wgp", [128, 8, D], BF16)
        wsp = P.sb("wsp", [128, 16, D], BF16)
        wop = P.sb("wop", [128, 8, D], BF16)
        P.dma("pool", wgp[:], gla_proj[l].rearrange("(k p) n -> p k n", p=128))
        P.dma("pool", wsp[:], ssd_proj[l].rearrange("(k p) n -> p k n", p=128))
        P.dma("pool", wop[:], w_out[l].rearrange("(k p) n -> p k n", p=128))
        w2 = P.sb("w2", [16, 512], F32)
        P.dma("sp", w2[:], gk_w2[l])
        gkbT = P.sb("gkbT", [128, 4], F32)
        ngkbT = P.sb("ngkbT", [128, 4], F32)
        loadT(gkbT[:], gk_b[l], 4, tmpT, pf[4])
        P.ts("dve", ngkbT[:], gkbT[:], -1.0, None, ALU.mult)
        cwT = P.sb("cwT", [128, 4, 24], F32)
        cbT_ = P.sb("cbT_", [128, 24], F32)
        for i in range(4):
            loadT(cwT[:, i, :], conv_w[l, i], 24, tmpT, pf[4])
        loadT(cbT_[:], conv_b[l], 24, tmpT, pf[4])
        gnw = P.sb("gnw", [64, 256], F32)
        P.dma("sp", gnw[:], gla_norm_w[l].pbc(64))
        snw = P.sb("snw", [64, 2048], F32)
        P.dma("sp", snw[:], ssd_norm_w[l].pbc(64))
        dtb = P.sb("dtb", [64, 32], F32)
        P.dma("sp", dtb[:], dt_bias[l].pbc(64))
        Aneg = P.sb("Aneg", [64, 32], F32)
        P.dma("sp", Aneg[:], A_log[l].pbc(64))
        P.act(Aneg[:], Aneg[:], AF.Exp)
        P.ts("dve", Aneg[:], Aneg[:], -1.0, None, ALU.mult)
        Dbc = P.sb("Dbc", [64, 32], F32)
        P.dma("sp", Dbc[:], ssd_D[l].pbc(64))
        Sg = P.sb("Sg", [128, 4, 256], F32)
        Sgb = P.sb("Sgb", [128, 4, 256], BF16)
        Ss = P.sb("Ss", [128, 2048], F32)
        Ssb = P.sb("Ssb", [128, 2048], BF16)
        cst = P.sb("cst", [72, 128], F32)
        qk = P.sb("qk", [128, 8, 64], BF16)
        gkl = P.sb("gkl", [16, 64], F32)
        vg = P.sb("vg", [64, 2048], BF16)
        zt = P.sb("zt", [64, 2048], BF16)
        dtr = P.sb("dtr", [64, 32], F32)
        xbc = P.sb("xbc", [128, 24, 67], BF16)
        ee = P.sb("ee", [128, 4, 64], F32)
        ll = P.sb("ll", [128, 4, 64], F32)
        bcs = P.sb("bcs", [128, 4, 64], F32)
        epos = P.sb("epos", [128, 4, 64], F32)
        eneg = P.sb("eneg", [128, 4, 64], F32)
        qt = P.sb("qt", [128, 4, 64], BF16)
        ktf = P.sb("ktf", [128, 4, 64], F32)
        kt = P.sb("kt", [128, 4, 64], BF16)
        kendT = P.sb("kendT", [128, 4, 64], BF16)
        kend = P.sb("kend", [64, 4, 128], BF16)
        attT = P.sb("attT", [64, 4, 64], BF16)
        ssg = P.sb("ssg", [64, 4], F32)
        sgo = P.sb("sgo", [64, 1024], BF16)
        oa = P.sb("oa", [64, 1024], BF16)
        oaT = P.sb("oaT", [128, 8, 128], BF16)
        yT = P.sb("yT", [128, 16, 128], BF16)
        xact = P.sb("xact", [128, 24, 64], BF16)
        xs_tm = P.sb("xs_tm", [64, 2048], BF16)
        B_tm = P.sb("B_tm", [64, 512], BF16)
        dtv = P.sb("dtv", [64, 32], F32)
        dtA = P.sb("dtA", [64, 32], F32)
        expa = P.sb("expa", [64, 32], F32)
        wend = P.sb("wend", [64, 32], F32)
        decS = P.sb("decS", [128, 32], F32)
        rhsD = P.sb("rhsD", [64, 32, 64], F32)
        E_ = rhsD
        cbm = P.sb("cbm", [64, 4, 64], F32)
        WT = P.sb("WT", [64, 32, 64], BF16)
        xdt = P.sb("xdt", [64, 2048], BF16)
        xw = P.sb("xw", [64, 2048], BF16)
        ytmp = P.sb("ytmp", [64, 512], F32)
        yvb = P.sb("yv", [128, 2048], F32)
        ssy = P.sb("ssy", [64, 4], F32)
        yn = P.sb("yn", [64, 2048], BF16)
        gts = P.sb("gts", [128, 16, 128], BF16)
        gsg = P.sb("gsg", [128, 16, 128], BF16)
        bra = P.sb("bra", [128, 8, 128], BF16)
        mT = P.sb("mT", [128, 8, 128], BF16)
        xt_ = P.sb("xt_", [128, 8, 128], F32)
        cvo = P.sb("cvo", [128, 3, 24], F32)
        cvs = P.sb("cvs", [72, 128], F32)
        yv = yvb[0:64, :]
        y2b = P.sb("y2", [128, 2048], F32)
        y2 = y2b[0:64, :]
        sz = y2
        on = P.sb("on", [64, 1024], BF16)
        junkg = ee[0:64, :, :].rearrange("p h t -> p (h t)")
        junk = ytmp
        dtmp = y2b[:, 0:13 * 64].rearrange("p (a t) -> p a t", t=64)
        cacc = yvb[:, 0:1536].rearrange("p (a t) -> p a t", t=64)
        cacc2 = P.sb("cacc2", [128, 11, 64], F32)
        ptmp = P.sb("ptmp", [128, 11, 64], F32)
        sst = yvb[:, :].rearrange("p (a n) -> p a n", n=128)
        featq = featT_d[0:1024, :].rearrange("(h p) t -> p h t", p=128)
        featx = featT_d[1024:4096, :].rearrange("(h p) t -> p h t", p=128)
        featg = featT_d[4096:6144, :].rearrange("(h p) t -> p h t", p=128)

        def gla_chain(c, s, t0, half):
            for h in range(4):
                P.mm(pf[0][:, h * 64:(h + 1) * 64], w2[:, h * 128:(h + 1) * 128], gkl[:], start=True, stop=True)
            yield
            for h in range(4):
                P.act(ee[:, h, :], pf[0][:, h * 64:(h + 1) * 64], AF.Exp, bias=ngkbT[:, h:h + 1], scale=-1.0)
            P.act(ll[:], ee[:], AF.Ln, bias=ones_f[:, 0:1], scale=1.0)
            yield
            for h in range(4):
                P.op("dve", lambda E, o=bcs[:, h, :].ap, a=ones_f[:, 0:64].ap, b=ll[:, h, :].ap:
                     E.tensor_tensor_scan(o, a, b, 0.0, ALU.mult, ALU.add), reads=[ones_f, ll], writes=[bcs])
            yield
            P.act(epos[:], bcs[:], AF.Exp, scale=-1.0 / 16.0)
            P.act(eneg[:], bcs[:], AF.Exp, scale=1.0 / 16.0)
            yield
            P.stt("dve", qt[:], qk[:, 0:4, :], float(128 ** -0.5), epos[:], ALU.mult, ALU.mult)
            P.tt("dve", ktf[:], qk[:, 4:8, :], eneg[:], ALU.mult)
            yield
            P.cp("act", kt[:], ktf[:])
            P.tt("dve", kendT[:], ktf[:], epos[:, :, 63:64].bc([128, 4, 64]), ALU.mult)
            yield
            for h in range(4):
                P.mm(pf[0][0:64, 256 + h * 64:256 + (h + 1) * 64], kt[:, h, :], qt[:, h, :], start=True, stop=True)
            for h in range(4):
                P.tr(pb[0][0:64, h * 128:(h + 1) * 128], kendT[:, h, :], ident_b[:])
            yield
            P.tt("dve", attT[:], pf[0][0:64, 256:512].rearrange("p (h i) -> p h i", i=64),
                 triU[:].unsqueeze(1).bc([64, 4, 64]), ALU.mult)
            P.cp("act", kend[:], pb[0][0:64, 0:512].rearrange("p (h d) -> p h d", d=128))
            yield
            for h in range(4):
                ob = pf[1 + h // 2][0:64, (h % 2) * 256:(h % 2 + 1) * 256]
                P.mm(ob, attT[:, h, :], vg[:, h * 256:(h + 1) * 256], start=True, stop=False)
                P.mm(ob, qt[:, h, :], Sgb[:, h, :], start=False, stop=True)
            yield
            P.memset("pool", ssg[:], 0.0)
            for h in range(4):
                ob = pf[1 + h // 2][0:64, (h % 2) * 256:(h % 2 + 1) * 256]
                P.act(junkg, ob, AF.Square, accum=ssg[:, h:h + 1])
            yield
            P.ts("dve", ssg[:], ssg[:], 1.0 / 256, EPS, ALU.mult, ALU.add)
            P.tt("pool", ssg[:], ssg[:], neghalf[0:64, 0:1].bc([64, 4]), ALU.pow)
            yield
            for h in range(4):
                ob = pf[1 + h // 2][0:64, (h % 2) * 256:(h % 2 + 1) * 256]
                P.stt("dve", on[:, h * 256:(h + 1) * 256], ob, ssg[:, h:h + 1], gnw[:], ALU.mult, ALU.mult)
                if h % 2 == 1:
                    yield
            for h in range(4):
                db = pf[1 + h // 2][:, (h % 2) * 256:(h % 2 + 1) * 256]
                P.mm(db, kend[:, h, :], vg[:, h * 256:(h + 1) * 256], start=True, stop=True)
            P.tt("pool", oa[:], on[:], sgo[:], ALU.mult)
            yield
            for h in range(4):
                db = pf[1 + h // 2][:, (h % 2) * 256:(h % 2 + 1) * 256]
                P.stt("dve", Sg[:, h, :], Sg[:, h, :], epos[:, h, 63:64], db, ALU.mult, ALU.add)
                if h % 2 == 1:
                    yield
            P.cp("act", Sgb[:], Sg[:])
            if cfg.last_chunk(c):
                P.dma("sp", ogla_d[l, s].rearrange("h d e -> d h e"), Sg[:])
            for k in range(8):
                P.tr(pb[0][:, 512 + k * 64:512 + (k + 1) * 64], oa[:, k * 128:(k + 1) * 128], ident_b[0:64, 0:64])
            yield
            P.cp("act", oaT[:, :, half:half + 64], pb[0][:, 512:1024].rearrange("p (k t) -> p k t", t=64))
            yield

        def ssd_chain(c, s, t0, half):
            ND = 13
            bcd = lambda v: v.unsqueeze(2).bc([128, ND, 64])
            bcp = lambda v: v.unsqueeze(2).bc([128, 24 - ND, 64])
            P.tt("dve", cacc[:, 0:ND, :], xbc[:, 0:ND, 3:67], bcd(cwT[:, 3, 0:ND]), ALU.mult)
            P.tt("pool", cacc2[:], xbc[:, ND:24, 3:67], bcp(cwT[:, 3, ND:24]), ALU.mult)
            yield
            P.tt("dve", cacc[:, 0:ND, :], cacc[:, 0:ND, :], bcd(cbT_[:, 0:ND]), ALU.add)
            P.tt("pool", cacc2[:], cacc2[:], bcp(cbT_[:, ND:24]), ALU.add)
            yield
            for i in range(3):
                P.tt("dve", dtmp[:], xbc[:, 0:ND, i:i + 64], bcd(cwT[:, i, 0:ND]), ALU.mult)
                P.tt("pool", ptmp[:], xbc[:, ND:24, i:i + 64], bcp(cwT[:, i, ND:24]), ALU.mult)
                yield
                P.tt("dve", cacc[:, 0:ND, :], cacc[:, 0:ND, :], dtmp[:], ALU.add)
                P.tt("pool", cacc2[:], cacc2[:], ptmp[:], ALU.add)
                yield
            P.act(xact[:, 0:ND, :], cacc[:, 0:ND, :], AF.Silu)
            P.act(xact[:, ND:24, :], cacc2[:], AF.Silu)
            P.act(zt[:], zt[:], AF.Silu)
            P.act(sgo[:], vg[:, 1024:2048], AF.Silu)
            P.tt("dve", dtA[:], dtv[:], Aneg[:], ALU.mult)
            yield
            P.mm(pf[3][0:64, 0:32], triU[:], dtA[:], start=True, stop=True)
            P.mm(pf[3][0:64, 32:64], Lst[:], dtA[:], start=True, stop=True)
            P.mm(pf[3][:, 64:96], ones_f[0:64, :], dtA[:], start=True, stop=True)
            if cfg.last_chunk(c):
                P.cp("dve", cvo[:], xbc[:, :, 64:67].rearrange("p a t -> p t a"))
                P.tr(pf[3][0:72, 128:256], cvo[:].rearrange("p t a -> p (t a)"), ident_f[:])
                P.cp("dve", cvs[:], pf[3][0:72, 128:256])
                P.dma("sp", oconv_d[l, s].rearrange("t (a p) -> (t a) p", p=128), cvs[:])
            yield
            P.act(expa[:], pf[3][0:64, 0:32], AF.Exp)
            P.act(wend[:], pf[3][0:64, 32:64], AF.Exp)
            P.act(decS[:], pf[3][:, 64:96], AF.Exp)
            P.tt("pool", rhsD[:], dtA[:].unsqueeze(2).bc([64, 32, 64]), triU[:].unsqueeze(1).bc([64, 32, 64]), ALU.mult)
            yield
            P.tt("dve", wend[:], wend[:], dtv[:], ALU.mult)
            for rnd in range(2):
                for a in range(8):
                    P.tr(pb[1][0:64, a * 128:(a + 1) * 128], xact[:, rnd * 8 + a, :], ident_b[:])
                yield
                P.cp("act", xs_tm[:, rnd * 1024:(rnd + 1) * 1024], pb[1][0:64, :])
                yield
            for a in range(4):
                P.tr(pb[1][0:64, a * 128:(a + 1) * 128], xact[:, 16 + a, :], ident_b[:])
            for g in range(4):
                P.mm(pf[3][0:64, 256 + g * 64:256 + (g + 1) * 64], xact[:, 16 + g, :], xact[:, 20 + g, :], start=True, stop=True)
            yield
            P.cp("act", B_tm[:], pb[1][0:64, 0:512])
            P.tt("dve", cbm[:], pf[3][0:64, 256:512].rearrange("p (g i) -> p g i", i=64),
                 triU[:].unsqueeze(1).bc([64, 4, 64]), ALU.mult)
            yield
            xs3 = xs_tm[:].rearrange("p (h q) -> p h q", q=64)
            P.tt("pool", xdt[:].rearrange("p (h q) -> p h q", q=64), xs3, dtv[:].unsqueeze(2).bc([64, 32, 64]), ALU.mult)
            yield
            P.tt("pool", xw[:].rearrange("p (h q) -> p h q", q=64), xs3, wend[:].unsqueeze(2).bc([64, 32, 64]), ALU.mult)
            yield
            P.tt("pool", y2.rearrange("p (h q) -> p h q", q=64), xs3, Dbc[:].unsqueeze(2).bc([64, 32, 64]), ALU.mult)
            rhsDf = rhsD[:].rearrange("p h i -> p (h i)")
            Ef = E_[:].rearrange("p h i -> p (h i)")
            E4 = E_[:].rearrange("p (g a) i -> p g a i", a=8)
            W4 = WT[:].rearrange("p (g a) i -> p g a i", a=8)
            def d_exp(g):
                P.mm(pf[3][0:64, :], Lst[:], rhsDf[:, g * 512:(g + 1) * 512], start=True, stop=True)
                P.act(Ef[:, g * 512:(g + 1) * 512], pf[3][0:64, :], AF.Exp)

            def wt(g):
                P.tt("dve", W4[:, g], E4[:, g], cbm[:, g, :].unsqueeze(1).bc([64, 8, 64]), ALU.mult)

            d_exp(0)
            yield
            wt(0)
            yield
            for g in range(4):
                X = pf[4]
                Y = pf[5]
                if g + 1 < 4:
                    d_exp(g + 1)
                for a in range(8):
                    h = g * 8 + a
                    P.mm(X[0:64, a * 64:(a + 1) * 64], WT[:, h, :], xdt[:, h * 64:(h + 1) * 64], start=True, stop=True)
                P.mm(Y[0:64, :], xact[:, 20 + g, :], Ssb[:, g * 512:(g + 1) * 512], start=True, stop=True)
                yield
                if g + 1 < 4:
                    wt(g + 1)
                    yield
                P.tt("dve", ytmp[:].rearrange("p (a q) -> p a q", q=64), Y[0:64, :].rearrange("p (a q) -> p a q", q=64),
                     expa[:, g * 8:(g + 1) * 8].unsqueeze(2).bc([64, 8, 64]), ALU.mult)
                yield
                P.tt("dve", yv[:, g * 512:(g + 1) * 512], ytmp[:], X[0:64, :], ALU.add)
                yield
            for g in range(4):
                P.mm(pf[4 + g % 2][:, :], B_tm[:, g * 128:(g + 1) * 128], xw[:, g * 512:(g + 1) * 512], start=True, stop=True)
                if g == 0:
                    P.tt("pool", Ss[:].rearrange("p (h q) -> p h q", q=64), Ss[:].rearrange("p (h q) -> p h q", q=64),
                         decS[:].unsqueeze(2).bc([128, 32, 64]), ALU.mult)
                yield
                P.tt("dve", Ss[:, g * 512:(g + 1) * 512], Ss[:, g * 512:(g + 1) * 512], pf[4 + g % 2][:, :], ALU.add)
                yield
            P.cp("act", Ssb[:], Ss[:])
            P.tt("pool", yv, yv, y2, ALU.add)
            yield
            P.tt("dve", yv, yv, zt[:], ALU.mult)
            P.memset("pool", ssy[:], 0.0)
            yield
            for g in range(4):
                P.act(junk[:], yv[:, g * 512:(g + 1) * 512], AF.Square, accum=ssy[:, g:g + 1])
            yield
            P.ts("dve", ssy[:], ssy[:], 1.0 / 512, EPS, ALU.mult, ALU.add)
            P.tt("pool", ssy[:], ssy[:], neghalf[0:64, 0:1].bc([64, 4]), ALU.pow)
            yield
            for g in range(4):
                P.stt("dve", yn[:, g * 512:(g + 1) * 512], yv[:, g * 512:(g + 1) * 512],
                      ssy[:, g:g + 1], snw[:, g * 512:(g + 1) * 512], ALU.mult, ALU.mult)
                if g % 2 == 1:
                    yield
            for a in range(16):
                P.tr(pb[1][:, a * 64:(a + 1) * 64], yn[:, a * 128:(a + 1) * 128], ident_b[0:64, 0:64])
            yield
            P.cp("act", yT[:, :, half:half + 64], pb[1][:, :].rearrange("p (a t) -> p a t", t=64))
            if cfg.last_chunk(c):
                for a in range(16):
                    pk = pf[4 + a % 2]
                    P.tr(pk[:, 0:128], Ss[:, a * 128:(a + 1) * 128], ident_f[:])
                    evac(sst[:, a, :], pk[:, 0:128])
                P.dma("sp", ossd_d[l, s].rearrange("(a p) n -> p a n", p=128), sst)
            yield

        def chunk_loads(c):
            s = cfg.seq_of_chunk(c)
            t0 = c * 64
            first = cfg.first_chunk(c)
            P.dma("sp", qk[:], featq[:, :, t0:t0 + 64])
            P.dma("sp", gkl[:], gkl_d[:, t0:t0 + 64])
            P.dma("sp", vg[:], tokM_d[t0:t0 + 64, 0:2048])
            P.dma("sp", zt[:], tokM_d[t0:t0 + 64, 2048:4096])
            P.dma("sp", dtv[:], dt_d[t0:t0 + 64, :])
            if first:
                P.dma("sp", xbc[:, :, 3:67], featx[:, :, t0:t0 + 64])
                if s == 0:
                    P.memset("pool", xbc[:, :, 0:3], 0.0)
                    P.memset("pool", Sg[:], 0.0)
                    P.memset("pool", Sgb[:], 0.0)
                    P.memset("pool", Ss[:], 0.0)
                    P.memset("pool", Ssb[:], 0.0)
                else:
                    P.dma("sp", cst[:], sconv_d[l, s - 1].rearrange("t (a p) -> (t a) p", p=128))
                    P.tr(pf[4][:, 0:72], cst[:], ident_f[0:72, 0:72])
                    P.cp("dve", xbc[:, :, 0:3], pf[4][:, 0:72].rearrange("p (t a) -> p a t", a=24))
                    P.dma("sp", Sg[:], sgla_d[l, s - 1].rearrange("h d e -> d h e"))
                    P.cp("act", Sgb[:], Sg[:])
                    P.dma("sp", sst, sssd_d[l, s - 1].rearrange("(a p) n -> p a n", p=128))
                    for a in range(16):
                        pk = pf[a % 4]
                        P.tr(pk[:, 0:128], sst[:, a, :], ident_f[:])
                        evac(Ss[:, a * 128:(a + 1) * 128], pk[:, 0:128])
                    P.cp("act", Ssb[:], Ss[:])
            else:
                P.dma("sp", xbc[:], featx[:, :, t0 - 3:t0 + 64])

        for c in range(NCH):
            s = cfg.seq_of_chunk(c)
            t0 = c * 64
            half = (c % 2) * 64
            first = cfg.first_chunk(c)
            if c == 0:
                chunk_loads(0)
            ga = gla_chain(c, s, t0, half)
            sa = ssd_chain(c, s, t0, half)
            live = [sa, ga]
            while live:
                for gen in list(live):
                    try:
                        next(gen)
                    except StopIteration:
                        live.remove(gen)
            if c + 1 < NCH:
                chunk_loads(c + 1)
            if c % 2 == 1:
                tt0 = (c // 2) * 128
                P.dma("sp", gts[:], featg[:, :, tt0:tt0 + 128])
                P.act(gsg[:], gts[:], AF.Sigmoid)
                P.dma("sp", xt_[:], xT_d[:].rearrange("(k p) t -> p k t", p=128)[:, :, tt0:tt0 + 128])
                for n in range(8):
                    pa = pf[n // 4][:, (n % 4) * 128:(n % 4 + 1) * 128]
                    for k in range(8):
                        P.mm(pa, wgp[:, k, n * 128:(n + 1) * 128], oaT[:, k, :], start=(k == 0), stop=(k == 7))
                for n in range(8):
                    pbk = pf[2 + n // 4][:, (n % 4) * 128:(n % 4 + 1) * 128]
                    for k in range(16):
                        P.mm(pbk, wsp[:, k, n * 128:(n + 1) * 128], yT[:, k, :], start=(k == 0), stop=(k == 15))
                for hh in range(2):
                    P.tt("dve", bra[:, hh * 4:(hh + 1) * 4, :], pf[hh][:, :].rearrange("p (n t) -> p n t", t=128),
                         gsg[:, hh * 4:(hh + 1) * 4, :], ALU.mult)
                    P.tt("dve", gsg[:, 8 + hh * 4:12 + hh * 4, :], pf[2 + hh][:, :].rearrange("p (n t) -> p n t", t=128),
                         gsg[:, 8 + hh * 4:12 + hh * 4, :], ALU.mult)
                P.tt("dve", mT[:], bra[:], gsg[:, 8:16, :], ALU.add)
                for n in range(8):
                    py = pf[4 + n // 4][:, (n % 4) * 128:(n % 4 + 1) * 128]
                    for k in range(8):
                        P.mm(py, wop[:, k, n * 128:(n + 1) * 128], mT[:, k, :], start=(k == 0), stop=(k == 7))
                for n in range(8):
                    py = pf[4 + n // 4][:, (n % 4) * 128:(n % 4 + 1) * 128]
                    for (s0, ln, sq_) in cfg.segments(tt0, tt0 + 128):
                        P.stt("dve", xt_[:, n, s0 - tt0:s0 - tt0 + ln], py[:, s0 - tt0:s0 - tt0 + ln],
                              modG1[:, l, n, sq_:sq_ + 1], xt_[:, n, s0 - tt0:s0 - tt0 + ln], ALU.mult, ALU.add)
                P.dma("sp", xT_d[:].rearrange("(k p) t -> p k t", p=128)[:, :, tt0:tt0 + 128], xt_[:])
        P.pop_scope()

    def final_norm():
        P.push_scope()
        xg = [P.sb("fxg%d" % i, [128, 8, 128], F32) for i in range(2)]
        sq = P.sb("fsq", [128, 8, 128], F32)
        rstd = P.sb("frstd", [128, 128], F32)
        tmpn = P.sb("ftmp", [128, 8, 128], F32)
        yo = [P.sb("fyo%d" % i, [128, D], F32) for i in range(2)]
        for t in range(NTL):
            xv = xg[t % 2]
            P.dma("sp", xv[:], xT_d[:].rearrange("(k p) t -> p k t", p=128)[:, :, t * 128:(t + 1) * 128])
            P.act(sq[:], xv[:], AF.Square)
            for k in range(8):
                P.mm(pf[0][:, 0:128], ones_f[:], sq[:, k, :], start=(k == 0), stop=(k == 7))
            P.act(rstd[:], pf[0][:, 0:128], AF.Sqrt, bias=epsb[:, 0:1], scale=1.0 / D)
            P.op("dve", lambda E, o=rstd[:].ap: E.reciprocal(o, o), reads=[rstd], writes=[rstd])
            P.tt("dve", tmpn[:], xv[:], rstd[:].unsqueeze(1).bc([128, 8, 128]), ALU.mult)
            P.tt("dve", tmpn[:], tmpn[:], fnw[:].unsqueeze(2).bc([128, 8, 128]), ALU.mult)
            yv_ = yo[t % 2]
            for k in range(8):
                pk = pf[1 + k % 4]
                P.tr(pk[:, 0:128], tmpn[:, k, :], ident_f[:])
                evac(yv_[:, k * 128:(k + 1) * 128], pk[:, 0:128])
            P.dma("sp", y_d[t * 128:(t + 1) * 128, :], yv_[:])
        P.pop_scope()

    PEER = peer_builder(P, cfg, locals())

    for l in range(DEPTH):
        P.push_scope()
        hTbox[0] = P.sb("hT", [128, 8, NT], BF16)
        norm_mod(modA1, modB1, l)
        in_proj(l)
        P.pop_scope()
        mixers(l)
        if cfg.do_peer:
            P.push_scope()
            hTbox[0] = P.sb("hT2", [128, 8, NT], BF16)
            norm_mod(modA2, modB2, l)
            PEER(l)
            P.pop_scope()
    final_norm()
    P.finish(outs)
    P.run_block()
    P.close()
    return P


def peer_builder(P, cfg, env):
    NT, NTL = cfg.NT, cfg.NTL
    pf, pb = env["pf"], env["pb"]
    hTbox = env["hTbox"]
    ident_f, ident_b, iota_f = env["ident_f"], env["ident_b"], env["iota_f"]
    peer_wq, keys1, keys2, peer_u, peer_v = env["peer_wq"], env["keys1"], env["keys2"], env["peer_u"], env["peer_v"]
    uT_d, vb_d, xT_d, modG2 = env["uT_d"], env["vb_d"], env["xT_d"], env["modG2"]
    rr = [0]

    def evac(o, i):
        rr[0] += 1
        P.cp("act" if rr[0] % 2 else "dve", o, i)

    def PEER(l):
        hT = hTbox[0]
        P.push_scope()
        i1T = P.sb("i1T", [128, NT], BF16)
        i2T = P.sb("i2T", [128, NT], BF16)
        gT = P.sb("gT", [128, NT], BF16)
        iota_b = P.sb("iota_b", [128, 128], BF16)
        P.cp("dve", iota_b[:], iota_f[:])
        P.push_scope()
        ub = [P.sb("ub%d" % i, [128, 4, D], BF16) for i in range(2)]
        us = [P.sb("us%d" % i, [128, 4, 8, 128], BF16) for i in range(2)]
        vs = [P.sb("vs%d" % i, [128, 8, D], BF16) for i in range(2)]

        def pre_u(b):
            u_ = ub[b % 2]
            s_ = us[b % 2]
            P.dma("pool", u_[:], peer_u[l, :, b * 4:(b + 1) * 4, :])
            for ii in range(4):
                pk = pb[ii % 2]
                for k in range(8):
                    P.tr(pk[:, k * 128:(k + 1) * 128], u_[:, ii, k * 128:(k + 1) * 128], ident_b[:])
                P.cp("act", s_[:, ii, :, :], pk[:, :].rearrange("p (k i) -> p k i", i=128))
            P.dma("sp", uT_d[b * 4:(b + 1) * 4].rearrange("i p k j -> p i k j"), s_[:])

        def pre_v(b):
            v_ = vs[b % 2]
            P.dma("pool", v_[:], peer_v[l, :, b * 8:(b + 1) * 8, :])
            P.dma("sp", vb_d[:, b * 8:(b + 1) * 8, :], v_[:])

        pre_jobs = []
        for b in range(32):
            pre_jobs.append((pre_u, b))
            if b % 2 == 1:
                pre_jobs.append((pre_v, b // 2))
        per_tile = -(-len(pre_jobs) // NTL)
        wq = P.sb("wq", [128, 8, 2048], BF16)
        P.dma("pool", wq[:], peer_wq[l].rearrange("(k p) n -> p k n", p=128))
        kT = P.sb("kT", [128, 16, 128], BF16)
        ktmp = P.sb("ktmp", [128, 128], F32)
        for h in range(8):
            for wh, kd in enumerate((keys1, keys2)):
                P.dma("sp", ktmp[:], kd[l, h])
                P.tr(pf[4][:, 0:128], ktmp[:], ident_f[:])
                P.cp("dve", kT[:, h * 2 + wh, :], pf[4][:, 0:128])
        qT = P.sb("qT", [128, 16, 128], BF16)
        sc = P.sb("sc", [128, 16, 128], F32)
        wk = P.sb("wk", [128, 256], F32)
        v12 = P.sb("v12", [128, 16, 16], F32)
        x12 = P.sb("x12", [128, 16, 16], U32)
        x12f = P.sb("x12f", [128, 16, 16], F32)
        cand = P.sb("cand", [128, 8, 256], F32)
        s16 = P.sb("s16", [128, 8, 16], F32)
        p16 = P.sb("p16", [128, 8, 16], U32)
        pau = P.sb("pau", [128, 8, 16], U32)
        pbu = P.sb("pbu", [128, 8, 16], U32)
        af = P.sb("af", [128, 8, 16], F32)
        bf_ = P.sb("bf_", [128, 8, 16], F32)
        oh = P.sb("oh", [128, 8, 16, 16], F32)
        i1f = P.sb("i1f", [128, 8, 16], F32)
        i2f = P.sb("i2f", [128, 8, 16], F32)
        esum = P.sb("esum", [128, 8], F32)
        gsm = P.sb("gsm", [128, 8, 16], F32)
        for t in range(NTL):
            ts0 = t * 128
            for j in range(16):
                pk = pf[j % 4]
                for k in range(8):
                    P.mm(pk[:, 0:128], wq[:, k, j * 128:(j + 1) * 128], hT[:, k, ts0:ts0 + 128], start=(k == 0), stop=(k == 7))
                evac(qT[:, j, :], pk[:, 0:128])
            for j in range(16):
                pk = pf[j % 4]
                P.mm(pk[:, 0:128], qT[:, j, :], kT[:, j, :], start=True, stop=True)
                evac(sc[:, j, :], pk[:, 0:128])

            def top16(src, vals, idx, width):
                P.op("dve", lambda E, o=vals[:, 0:8].ap, i=src.ap: E.max(out=o, in_=i), reads=[src], writes=[vals])
                P.op("dve", lambda E, o=idx[:, 0:8].ap, m=vals[:, 0:8].ap, i=src.ap: E.max_index(out=o, in_max=m, in_values=i),
                     reads=[src, vals], writes=[idx])
                P.op("dve", lambda E, o=wk[:, 0:width].ap, m=vals[:, 0:8].ap, i=src.ap:
                     E.match_replace(out=o, in_to_replace=m, in_values=i, imm_value=-1e30), reads=[src, vals], writes=[wk])
                P.op("dve", lambda E, o=vals[:, 8:16].ap, i=wk[:, 0:width].ap: E.max(out=o, in_=i), reads=[wk], writes=[vals])
                P.op("dve", lambda E, o=idx[:, 8:16].ap, m=vals[:, 8:16].ap, i=wk[:, 0:width].ap:
                     E.max_index(out=o, in_max=m, in_values=i), reads=[wk, vals], writes=[idx])

            for _ in range(per_tile):
                if pre_jobs:
                    fn_, b_ = pre_jobs.pop(0)
                    fn_(b_)
            for j in range(16):
                top16(sc[:, j, :], v12[:, j, :], x12[:, j, :], 128)
            P.cp("dve", x12f[:], x12[:])
            v4 = v12[:].rearrange("p (h w) a -> p h w a", w=2)
            x4 = x12f[:].rearrange("p (h w) a -> p h w a", w=2)
            P.tt("dve", cand[:].rearrange("p h (a b) -> p h a b", b=16), v4[:, :, 0, :].unsqueeze(3).bc([128, 8, 16, 16]),
                 v4[:, :, 1, :].unsqueeze(2).bc([128, 8, 16, 16]), ALU.add)
            for h in range(8):
                top16(cand[:, h, :], s16[:, h, :], p16[:, h, :], 256)
            P.ts("dve", pau[:], p16[:], 4, None, ALU.logical_shift_right)
            P.ts("dve", pbu[:], p16[:], 15, None, ALU.bitwise_and)
            P.cp("dve", af[:], pau[:])
            P.cp("dve", bf_[:], pbu[:])
            io16 = iota_f[:, 0:16].unsqueeze(1).unsqueeze(1).bc([128, 8, 16, 16])
            for (sel, xi, dst) in ((af, 0, i1f), (bf_, 1, i2f)):
                P.tt("dve", oh[:], sel[:].unsqueeze(3).bc([128, 8, 16, 16]), io16, ALU.is_equal)
                P.tt("pool", oh[:], oh[:], x4[:, :, xi, :].unsqueeze(2).bc([128, 8, 16, 16]), ALU.mult)
                P.op("dve", lambda E, o=dst[:].ap, i=oh[:].ap: E.tensor_reduce(o, i, AX.X, ALU.add), reads=[oh], writes=[dst])
            P.tt("dve", gsm[:], s16[:], s16[:, :, 0:1].bc([128, 8, 16]), ALU.subtract)
            P.act(gsm[:], gsm[:], AF.Exp)
            P.op("dve", lambda E, o=esum[:].ap, i=gsm[:].ap: E.tensor_reduce(o, i, AX.X, ALU.add), reads=[gsm], writes=[esum])
            P.op("dve", lambda E, o=esum[:].ap: E.reciprocal(o, o), reads=[esum], writes=[esum])
            P.tt("dve", gsm[:], gsm[:], esum[:].unsqueeze(2).bc([128, 8, 16]), ALU.mult)
            for (src, dstT) in ((i1f, i1T), (i2f, i2T), (gsm, gT)):
                P.tr(pf[5][:, 0:128], src[:].rearrange("p h k -> p (h k)"), ident_f[:])
                P.cp("act", dstT[:, ts0:ts0 + 128], pf[5][:, 0:128])
        P.pop_scope()

        TG = 256
        G = P.sb("G", [128, 128, TG], BF16)
        A8 = [P.sb("A8_%d" % i, [128, 16, 128], BF16) for i in range(2)]
        I1x = [P.sb("I1x_%d" % i, [128, 16, 128], BF16) for i in range(2)]
        iob16 = P.sb("iob16", [128, 16, 128], BF16)
        P.cp("dve", iob16[:], iota_b[:].unsqueeze(1).bc([128, 16, 128]))
        B8 = [P.sb("B8_%d" % i, [128, 16, 128], BF16) for i in range(2)]
        IB = 4
        uTb = [P.sb("uTb%d" % i, [128, IB, 8, 128], BF16) for i in range(2)]
        vbb = [P.sb("vbb%d" % i, [128, IB, D], BF16) for i in range(2)]
        ge = [P.sb("ge%d" % i, [128, TG], BF16) for i in range(2)]
        gw = [P.sb("gw%d" % i, [128, TG], BF16) for i in range(2)]
        xg = P.sb("pxg", [128, 8, TG], F32)
        iob = iota_b[:].unsqueeze(1).bc([128, 16, 128])
        for g0 in range(0, NT, TG):
            n = min(TG, NT - g0)
            def mat_i1(q):
                tkq = g0 + q * 16
                P.cp("act", I1x[q % 2][:], i1T[:, tkq:tkq + 16].unsqueeze(2).bc([128, 16, 128]))

            mat_i1(0)
            for q8 in range(n // 16):
                tk = g0 + q8 * 16
                a_ = A8[q8 % 2]
                b_ = B8[q8 % 2]
                ix_ = I1x[q8 % 2]
                if q8 + 1 < n // 16:
                    mat_i1(q8 + 1)
                P.tt("dve", a_[:], iob16[:], ix_[:], ALU.is_equal)
                P.tt("dve", b_[:], iob, i2T[:, tk:tk + 16].unsqueeze(2).bc([128, 16, 128]), ALU.is_equal)
                P.tt("pool", b_[:], b_[:], gT[:, tk:tk + 16].unsqueeze(2).bc([128, 16, 128]), ALU.mult)
                for hh in range(4):
                    pk = pf[hh]
                    pk3 = pk[:, :].rearrange("p (i t) -> p i t", t=4)
                    for u in range(4):
                        P.mm(pk3[:, :, u], a_[:, hh * 4 + u, :], b_[:, hh * 4 + u, :], start=True, stop=True)
                    tq = q8 * 16 + hh * 4
                    P.cp("act", G[:, :, tq:tq + 4], pk3)
            def load_blk(b):
                P.dma("sp", uTb[b % 2][:], uT_d[b * IB:(b + 1) * IB].rearrange("i p k j -> p i k j"))
                P.dma("sp", vbb[b % 2][:], vb_d[:, b * IB:(b + 1) * IB, :])

            def act_mm(i2):
                ut = uTb[(i2 // IB) % 2]
                pa = pf[4 + i2 % 2]
                for k in range(8):
                    P.mm(pa[:, 0:n], ut[:, i2 % IB, k, :], hT[:, k, g0:g0 + n], start=(k == 0), stop=(k == 7))

            load_blk(0)
            act_mm(0)
            for i2 in range(128):
                if i2 % IB == 0 and i2 // IB + 1 < 128 // IB:
                    load_blk(i2 // IB + 1)
                if i2 + 1 < 128:
                    act_mm(i2 + 1)
                pa = pf[4 + i2 % 2]
                vt = vbb[(i2 // IB) % 2]
                ge_ = ge[i2 % 2]
                gw_ = gw[i2 % 2]
                P.act(ge_[:, 0:n], pa[:, 0:n], AF.Gelu)
                P.tt("dve", gw_[:, 0:n], ge_[:, 0:n], G[:, i2, 0:n], ALU.mult)
                for j in range(8):
                    P.mm(pf[j // 2][:, (j % 2) * 256:(j % 2) * 256 + n], vt[:, i2 % IB, j * 128:(j + 1) * 128], gw_[:, 0:n],
                         start=(i2 == 0), stop=(i2 == 127))
            P.dma("sp", xg[:, :, 0:n], xT_d[:].rearrange("(k p) t -> p k t", p=128)[:, :, g0:g0 + n])
            for j in range(8):
                for (s0, ln, sq_) in cfg.segments(g0, g0 + n):
                    o0 = s0 - g0
                    P.stt("dve", xg[:, j, o0:o0 + ln], pf[j // 2][:, (j % 2) * 256 + o0:(j % 2) * 256 + o0 + ln],
                          modG2[:, l, j, sq_:sq_ + 1], xg[:, j, o0:o0 + ln], ALU.mult, ALU.add)
            P.dma("sp", xT_d[:].rearrange("(k p) t -> p k t", p=128)[:, :, g0:g0 + n], xg[:, :, 0:n])
        P.pop_scope()

    return PEER


from concourse.bass_utils import run_bass_kernel_spmd

_WNAMES = ["w_ada", "b_ada", "norm1_w", "w_in", "gla_gk_w2", "gla_gk_b", "gla_norm_w", "gla_proj", "ssd_conv_w",
           "ssd_conv_b", "ssd_dt_bias", "ssd_A_log", "ssd_D", "ssd_norm_w", "ssd_proj", "w_out", "norm2_w", "peer_wq",
           "peer_keys1", "peer_keys2", "peer_u", "peer_v", "final_norm_w"]


def _prep_weights(inp, depth):
    f = lambda a: np.ascontiguousarray(np.asarray(a, dtype=np.float32))
    w = {k: f(inp[k]) for k in _WNAMES}
    w["b_ada"] = w["b_ada"].reshape(depth, 48, 128)
    w["norm1_w"] = w["norm1_w"].reshape(depth, 8, 128)
    w["norm2_w"] = w["norm2_w"].reshape(depth, 8, 128)
    w["gla_gk_b"] = w["gla_gk_b"].reshape(depth, 4, 128)
    w["ssd_conv_w"] = w["ssd_conv_w"].reshape(depth, 4, 24, 128)
    w["ssd_conv_b"] = w["ssd_conv_b"].reshape(depth, 24, 128)
    w["peer_u"] = w["peer_u"].reshape(depth, 128, 128, 1024)
    w["peer_v"] = w["peer_v"].reshape(depth, 128, 128, 1024)
    w["final_norm_w"] = w["final_norm_w"].reshape(8, 128)
    return w


def run_cores(inp, n_cores, TP, NS, depth, do_peer=True):
    cfg = Cfg(TP, NS, depth, do_peer)
    nc = bass.Bass("TRN2", target_bir_lowering=False)
    build(nc, cfg)
    w = _prep_weights(inp, depth)
    f = lambda a: np.ascontiguousarray(np.asarray(a, dtype=np.float32))
    xp, xs = f(inp["x_prompt"]), f(inp["x_sample"])
    cp, cs = f(inp["c_prompt"]), f(inp["c_sample"])
    sg, ss, scv = f(inp["state_gla"]), f(inp["state_ssd"]), f(inp["state_conv"])
    in_maps = []
    for i in range(n_cores):
        m = dict(w)
        m["x"] = np.concatenate([xp[i], xs[i * NS:(i + 1) * NS].reshape(NS * 64, D)], axis=0)
        m["cvec"] = np.concatenate([cp[i:i + 1], cs[i * NS:(i + 1) * NS]], axis=0)
        m["sgla"] = np.ascontiguousarray(sg[:, i * NS:(i + 1) * NS])
        m["sssd"] = np.ascontiguousarray(ss[:, i * NS:(i + 1) * NS]).reshape(depth, NS, 2048, 128)
        m["sconv"] = np.ascontiguousarray(scv[:, i * NS:(i + 1) * NS])
        in_maps.append(m)
    res = run_bass_kernel_spmd(nc, in_maps, core_ids=list(range(n_cores)))
    B, BS = n_cores, n_cores * NS
    y_p = np.zeros((B, TP, D), np.float32)
    y_s = np.zeros((BS, 64, D), np.float32)
    gla_p = np.zeros((depth, B, 4, 128, 256), np.float32)
    ssd_p = np.zeros((depth, B, 32, 64, 128), np.float32)
    conv_p = np.zeros((depth, B, 3, 3072), np.float32)
    gla_s = np.zeros((depth, BS, 4, 128, 256), np.float32)
    ssd_s = np.zeros((depth, BS, 32, 64, 128), np.float32)
    conv_s = np.zeros((depth, BS, 3, 3072), np.float32)
    for i in range(n_cores):
        r = res.results[i]
        y = np.asarray(r["y"])
        y_p[i] = y[:TP]
        y_s[i * NS:(i + 1) * NS] = y[TP:].reshape(NS, 64, D)
        og = np.asarray(r["ogla"])
        os_ = np.asarray(r["ossd"]).reshape(depth, 1 + NS, 32, 64, 128)
        oc = np.asarray(r["oconv"])
        gla_p[:, i] = og[:, 0]
        ssd_p[:, i] = os_[:, 0]
        conv_p[:, i] = oc[:, 0]
        gla_s[:, i * NS:(i + 1) * NS] = og[:, 1:]
        ssd_s[:, i * NS:(i + 1) * NS] = os_[:, 1:]
        conv_s[:, i * NS:(i + 1) * NS] = oc[:, 1:]
    return (y_p, y_s, gla_p, ssd_p, conv_p, gla_s, ssd_s, conv_s)


def kernel(**inputs):
    return run_cores(inputs, 8, 2048, 4, 2)
```
